# Optimizing a Trainium2 kernel written in Bass

```python
import math
import jax, jax.numpy as jnp
from jax import lax
import numpy as np

D_MODEL = 1024
BATCH = 8
SEQ = 4096
DEPTH = 1

HEAD_DIM = 64
HEADS_PER_GROUP = 8
ATTN_GROUPS = ((128, 1), (512, 4), (2048, 16))
N_GROUPS = len(ATTN_GROUPS)
N_ATTN_HEADS = N_GROUPS * HEADS_PER_GROUP
ATTN_WIDTH = HEADS_PER_GROUP * HEAD_DIM
QKV_WIDTH = N_GROUPS * 3 * ATTN_WIDTH
BLOCK = 128
POOL_WINDOWS = (2, 4, 8, 16)
POOL_GROUPS = len(POOL_WINDOWS)
POOL_WIDTH = D_MODEL // 2
PGW = POOL_WIDTH // POOL_GROUPS
NUM_BUCKETS = 32
MAX_DISTANCE = 2048
EPS = 1e-6
SPLIT_SIZES = (QKV_WIDTH, ATTN_WIDTH, POOL_WIDTH, POOL_WIDTH, D_MODEL, D_MODEL)
SPLIT_POINTS = (QKV_WIDTH,
                QKV_WIDTH + ATTN_WIDTH,
                QKV_WIDTH + ATTN_WIDTH + POOL_WIDTH,
                QKV_WIDTH + ATTN_WIDTH + 2 * POOL_WIDTH,
                QKV_WIDTH + ATTN_WIDTH + 2 * POOL_WIDTH + D_MODEL)
IN_WIDTH = QKV_WIDTH + ATTN_WIDTH + 2 * POOL_WIDTH + 2 * D_MODEL

kernel_name = "hybrid_dilated_attn_pool_gated_block"


def rmsnorm(x, g):
    xf = x.astype(jnp.float32)
    y = xf * lax.rsqrt(jnp.mean(xf * xf, axis=-1, keepdims=True) + EPS)
    return (y * g.astype(jnp.float32)).astype(x.dtype)


def t5_bucket(n):
    max_exact = NUM_BUCKETS // 2
    nf = jnp.maximum(n, 1).astype(jnp.float32)
    large = max_exact + (jnp.log(nf / max_exact) / math.log(MAX_DISTANCE / max_exact)
                         * (NUM_BUCKETS - max_exact)).astype(jnp.int32)
    large = jnp.minimum(large, NUM_BUCKETS - 1)
    return jnp.where(n < max_exact, n, large)


def to_sub(t, dil):
    B, S = t.shape[:2]
    L = S // dil
    t = t.reshape((B, L, dil) + t.shape[2:])
    t = jnp.moveaxis(t, 2, 1)
    return t.reshape((B * dil, L) + t.shape[3:])


def from_sub(t, B, dil):
    L = t.shape[1]
    t = t.reshape((B, dil, L) + t.shape[2:])
    t = jnp.moveaxis(t, 1, 2)
    return t.reshape((B, L * dil) + t.shape[3:])


def dilated_window_attention(q, k, v, dil, n_back, bias_g):
    B, S, H, Dh = q.shape
    L = S // dil
    nb = -(-L // BLOCK)
    pad = nb * BLOCK - L
    Bd = B * dil

    def sub(t):
        return jnp.pad(to_sub(t, dil), ((0, 0), (0, pad), (0, 0), (0, 0)))

    def band(t):
        tp = jnp.pad(t, ((0, 0), (BLOCK, 0), (0, 0), (0, 0)))
        prev = tp[:, :-BLOCK].reshape(Bd, nb, BLOCK, H, Dh)
        cur = t.reshape(Bd, nb, BLOCK, H, Dh)
        return jnp.concatenate([prev, cur], axis=2)

    qb = sub(q).reshape(Bd, nb, BLOCK, H, Dh)
    kb = band(sub(k))
    vb = band(sub(v))

    i = jnp.arange(BLOCK)[:, None]
    j = jnp.arange(2 * BLOCK)[None, :]
    dist = BLOCK + i - j
    band_ok = (dist >= 0) & (dist <= n_back)
    key_ok = (jnp.arange(nb)[:, None, None] * BLOCK - BLOCK + j[None]) >= 0
    mask = band_ok[None] & key_ok
    bucket = t5_bucket(jnp.clip(dist, 0, n_back) * dil)
    bias = jnp.transpose(bias_g[bucket].astype(jnp.float32), (2, 0, 1))

    logits = jnp.einsum('znqhd,znkhd->znhqk', qb, kb).astype(jnp.float32) * (HEAD_DIM ** -0.5)
    logits = jnp.where(mask[None, :, None], logits + bias[None, None], -jnp.inf)
    m = jnp.max(logits, axis=-1, keepdims=True)
    p = jnp.exp(logits - m)
    denom = jnp.sum(p, axis=-1)
    o = jnp.einsum('znhqk,znkhd->znqhd', p.astype(vb.dtype), vb).astype(jnp.float32)
    o = o / jnp.moveaxis(denom, 2, 3)[..., None]
    lse = jnp.moveaxis(m[..., 0] + jnp.log(denom), 2, 3)

    o = o.reshape(Bd, nb * BLOCK, H, Dh)[:, :L]
    lse = lse.reshape(Bd, nb * BLOCK, H)[:, :L]
    return from_sub(o, B, dil), from_sub(lse, B, dil)


def multiscale_pool(u):
    B, S, C = u.shape
    uf = u.astype(jnp.float32)
    csp = jnp.pad(jnp.cumsum(uf, axis=1), ((0, 0), (1, 0), (0, 0)))
    t = jnp.arange(S)
    outs = []
    for g, win in enumerate(POOL_WINDOWS):
        cg = csp[:, :, g * PGW:(g + 1) * PGW]
        lo = jnp.maximum(t + 1 - win, 0)
        s = cg[:, 1:] - cg[:, lo]
        cnt = jnp.minimum(t + 1, win).astype(jnp.float32)
        outs.append(s / cnt[None, :, None])
    return jnp.concatenate(outs, axis=-1) - uf


def setup_inputs(seed: int = 0) -> dict:
    key = jax.random.key(seed)
    ks = jax.random.split(key, 14)
    f32 = jnp.float32
    nrm = lambda k, shape, s: (jax.random.normal(k, shape, f32) * s).astype(f32)
    return {
        "x": nrm(ks[0], (BATCH, SEQ, D_MODEL), 1.0),
        "c": nrm(ks[1], (BATCH, D_MODEL), 1.0),
        "norm_g": 1.0 + nrm(ks[2], (DEPTH, D_MODEL), 0.05),
        "w_ada": nrm(ks[3], (DEPTH, D_MODEL, 3 * D_MODEL), 0.5 * D_MODEL ** -0.5),
        "b_ada": nrm(ks[4], (DEPTH, 3 * D_MODEL), 0.01),
        "w_in": nrm(ks[5], (DEPTH, D_MODEL, IN_WIDTH), D_MODEL ** -0.5),
        "pool_w": nrm(ks[6], (DEPTH, POOL_GROUPS, PGW, PGW), PGW ** -0.5),
        "pool_scale": 1.0 + nrm(ks[7], (DEPTH, POOL_WIDTH), 0.1),
        "w_attn_br": nrm(ks[8], (DEPTH, ATTN_WIDTH, D_MODEL), ATTN_WIDTH ** -0.5),
        "w_pool_br": nrm(ks[9], (DEPTH, POOL_WIDTH, D_MODEL), POOL_WIDTH ** -0.5),
        "w_out": nrm(ks[10], (DEPTH, D_MODEL, D_MODEL), D_MODEL ** -0.5),
        "rel_bias": nrm(ks[11], (NUM_BUCKETS, N_ATTN_HEADS), 0.5),
        "final_g": 1.0 + nrm(ks[12], (D_MODEL,), 0.05),
    }


def reference(x, c, norm_g, w_ada, b_ada, w_in, pool_w, pool_scale, w_attn_br, w_pool_br, w_out, rel_bias, final_g):
    B, S, D = x.shape
    for l in range(DEPTH):
        mod = c @ w_ada[l] + b_ada[l]
        shift, scale, gate = jnp.split(mod, 3, axis=-1)
        h = rmsnorm(x, norm_g[l]) * (1.0 + scale[:, None]) + shift[:, None]

        proj = h @ w_in[l]
        qkv, z_attn, u_pool, z_pool, g_attn, g_pool = jnp.split(proj, SPLIT_POINTS, axis=-1)
        qkv = qkv.reshape(B, S, N_GROUPS, 3, HEADS_PER_GROUP, HEAD_DIM)

        outs, lses = [], []
        for gi, (win, dil) in enumerate(ATTN_GROUPS):
            bias_g = rel_bias[:, gi * HEADS_PER_GROUP:(gi + 1) * HEADS_PER_GROUP]
            o, lse = dilated_window_attention(qkv[:, :, gi, 0], qkv[:, :, gi, 1], qkv[:, :, gi, 2],
                                              dil, win // dil, bias_g)
            outs.append(o)
            lses.append(lse)
        wts = jax.nn.softmax(jnp.stack(lses, axis=0), axis=0)
        attn = jnp.sum(wts[..., None] * jnp.stack(outs, axis=0), axis=0)
        attn = attn.reshape(B, S, ATTN_WIDTH).astype(x.dtype)
        y_attn = (attn * jax.nn.silu(z_attn)) @ w_attn_br[l]

        pooled = multiscale_pool(u_pool).reshape(B, S, POOL_GROUPS, PGW)
        mixed = jnp.einsum('bsgc,gce->bsge', pooled, pool_w[l].astype(jnp.float32))
        mixed = (mixed.reshape(B, S, POOL_WIDTH) * pool_scale[l]).astype(x.dtype)
        y_pool = (mixed * jax.nn.silu(z_pool)) @ w_pool_br[l]

        merged = jax.nn.sigmoid(g_attn) * y_attn + jax.nn.sigmoid(g_pool) * y_pool
        x = x + gate[:, None] * (merged @ w_out[l])
    return rmsnorm(x, final_g)
```

```python
import math
from contextlib import ExitStack

import numpy as np
import concourse.bass as bass
import concourse.mybir as mybir
from concourse.bass_utils import run_bass_kernel_spmd

F32 = mybir.dt.float32
BF16 = mybir.dt.bfloat16
AF = mybir.ActivationFunctionType
ALU = mybir.AluOpType

S = 4096
D = 1024
NT = 32
NS = 8
EPS = 1e-6
DILS = (1, 4, 16)
EPOCH = 4000
SBUF_WORDS = 53200


class Op:
    __slots__ = ("eng", "fn", "slot", "slotn", "signal", "count", "deps", "waits")


class Prog:
    ENG = ("pe", "act", "dve", "pool", "sp")

    def __init__(self):
        self.ops = {e: [] for e in self.ENG}
        self.lastw = {}
        self.readers = {}
        self.slotcnt = {}
        self.lastdma = {}
        self.lastcomp = {}
        self.pending = {e: set() for e in self.ENG}

    def op(self, eng, fn, reads=(), writes=(), slot=None):
        o = Op()
        o.eng = eng
        o.fn = fn
        o.slot = slot
        o.signal = False
        o.count = 0
        o.slotn = 0
        deps = set(self.pending[eng])
        self.pending[eng] = set()
        for k in reads:
            w = self.lastw.get(k)
            if w is not None:
                deps.add(w)
        for k in writes:
            w = self.lastw.get(k)
            if w is not None:
                deps.add(w)
            rd = self.readers.get(k)
            if rd:
                for r in rd.values():
                    if isinstance(r, list):
                        deps.update(r)
                    else:
                        deps.add(r)
        for k in reads:
            rd = self.readers.setdefault(k, {})
            if slot is not None:
                rd.setdefault("_dma", []).append(o)
            else:
                rd[eng] = o
        for k in writes:
            self.lastw[k] = o
            self.readers[k] = {}
        deps.discard(o)
        o.deps = deps
        if slot is not None:
            n = self.slotcnt.get(slot, 0) + 1
            self.slotcnt[slot] = n
            o.slotn = n
            self.lastdma[slot] = o
        else:
            self.lastcomp[eng] = o
        self.ops[eng].append(o)
        return o

    def barrier(self):
        allops = set(self.lastcomp.values()) | set(self.lastdma.values())
        for e in self.ENG:
            self.pending[e] |= allops

    def finalize(self):
        for e in self.ENG:
            for o in self.ops[e]:
                for d in o.deps:
                    if d.slot is None and not (d.eng == "pe" and e == "pe"):
                        d.signal = True
        self.nsig = {}
        for e in self.ENG:
            c = 0
            for o in self.ops[e]:
                if o.slot is None and o.signal:
                    c += 1
                    o.count = c
            self.nsig[e] = c
        for e in self.ENG:
            seen = {}
            for o in self.ops[e]:
                w = {}
                for d in o.deps:
                    if d.slot is not None:
                        key = ("dma", d.slot)
                        val = d.slotn
                    else:
                        if d.eng == "pe" and e == "pe":
                            continue
                        key = ("eng", d.eng)
                        val = d.count
                    if seen.get(key, 0) >= val:
                        continue
                    if w.get(key, 0) < val:
                        w[key] = val
                for k, v in w.items():
                    seen[k] = v
                o.waits = w

    def emit(self, e, eng, sems):
        for o in self.ops[e]:
            for (kind, name), v in o.waits.items():
                if kind == "dma":
                    eng.wait_ge(sems["dma"][name], 16 * v)
                else:
                    ep = (v - 1) // EPOCH
                    eng.wait_ge(sems["eng"][name][ep], v - ep * EPOCH)
            if o.fn is None:
                continue
            inst = o.fn(eng)
            if o.slot is not None:
                inst.then_inc(sems["dma"][o.slot], 16)
            elif o.signal:
                ep = (o.count - 1) // EPOCH
                inst.then_inc(sems["eng"][e][ep], 1)


class Alloc:
    def __init__(self, big, nwords):
        self.big = big
        self.n = nwords
        self.top = 0
        self.hi = nwords

    def get_top(self, shape, dtype):
        save = self.top
        nel = 1
        for d in shape[1:]:
            nel *= d
        nw = ((nel * (4 if dtype == F32 else 2) + 3) // 4 + 7) // 8 * 8
        self.hi -= nw
        self.top = self.hi
        ap = self.get(shape, dtype, _chk=False)
        self.top = save
        assert self.top <= self.hi
        return ap

    def get(self, shape, dtype, _chk=True):
        nel = 1
        for d in shape[1:]:
            nel *= d
        nbytes = nel * (4 if dtype == F32 else 2)
        nw = (nbytes + 3) // 4
        off = self.top
        self.top += (nw + 7) // 8 * 8
        assert self.top <= (self.hi if _chk else self.n), f"SBUF overflow {self.top} > {self.hi}"
        ap = self.big[0:shape[0], off:off + nw]
        if dtype != F32:
            ap = ap.bitcast(dtype)
        if len(shape) > 2:
            names = "abcdef"[: len(shape) - 1]
            pat = "p (" + " ".join(names) + ") -> p " + " ".join(names)
            kw = {names[i]: shape[i + 1] for i in range(len(names))}
            ap = ap.rearrange(pat, **kw)
        return ap


def build(dbg=None):
    nc = bass.Bass("TRN2", target_bir_lowering=False)

    def din(name, shape):
        return nc.dram_tensor(name, shape, F32, kind="ExternalInput").ap()

    x = din("x", [S, D])
    cT_d = din("cT", [128, 8])
    ng_d = din("ng", [128, 8])
    wada_d = din("w_ada", [D, 3 * D])
    bada_d = din("b_ada", [1, 3 * D])
    win_d = din("w_in", [D, 8192])
    poolw_d = din("pool_w", [4, 128, 128])
    pscale_d = din("pscale", [128, 4])
    wab_d = din("w_ab", [512, D])
    wpb_d = din("w_pb", [512, D])
    wout_d = din("w_out", [D, D])
    bg_d = din("bg", [128, 12, 512])
    ident_d = din("ident", [128, 128])
    icnt_d = din("icnt", [128, 64])
    fg_d = din("fg", [128, D])
    y = nc.dram_tensor("y", [S, D], F32, kind="ExternalOutput").ap()
    dbg_d = None
    if dbg is not None:
        dshape = {"hT": ([128, 8 * S], BF16), "AT": ([128, 4 * S], BF16), "M2T": ([128, 4 * S], BF16),
                  "mg": ([128, 8 * S], BF16), "AT0": ([128, 4 * S], BF16), "acc": ([128, 2 * S], F32)}[dbg]
        dbg_d = nc.dram_tensor("dbg", dshape[0], dshape[1], kind="ExternalOutput").ap()

    win_v = win_d.rearrange("(kc p) f -> p kc f", p=128)

    P = Prog()
    es = ExitStack()
    with es:
        big = es.enter_context(nc.sbuf_tensor("big", [128, SBUF_WORDS], F32))
        psall = es.enter_context(nc.psum_tensor("psall", [128, 4096], F32))
        A = Alloc(big, SBUF_WORDS)

        def bank(b):
            return psall[:, b * 512:(b + 1) * 512]

        def bank_bf(b):
            return psall[:, b * 512:(b + 1) * 512].bitcast(BF16)

        def MM(out, lhsT, rhs, start, stop, reads, writes):
            P.op("pe", lambda e: e.matmul(out=out, lhsT=lhsT, rhs=rhs, start=start, stop=stop), reads, writes)

        def TR(out, in_, ident, reads, writes):
            P.op("pe", lambda e: e.transpose(out=out, in_=in_, identity=ident), reads, writes)

        def ACT(out, in_, func, reads, writes, scale=None, bias=None, accum_out=None):
            kw = {}
            if scale is not None:
                kw["scale"] = scale
            if bias is not None:
                kw["bias"] = bias
            if accum_out is not None:
                kw["accum_out"] = accum_out
            P.op("act", lambda e: e.activation(out=out, in_=in_, func=func, **kw), reads, writes)

        def TT(eng, out, in0, in1, op, reads, writes):
            P.op(eng, lambda e: e.tensor_tensor(out=out, in0=in0, in1=in1, op=op), reads, writes)

        def TS(eng, out, in0, s1, s2, op0, op1, reads, writes):
            if op1 is None:
                P.op(eng, lambda e: e.tensor_scalar(out=out, in0=in0, scalar1=s1, scalar2=None, op0=op0), reads, writes)
            else:
                P.op(eng, lambda e: e.tensor_scalar(out=out, in0=in0, scalar1=s1, scalar2=s2, op0=op0, op1=op1), reads, writes)

        def STT(out, in0, scalar, in1, op0, op1, reads, writes, accum_out=None):
            if accum_out is None:
                P.op("dve", lambda e: e.scalar_tensor_tensor(out=out, in0=in0, scalar=scalar, in1=in1, op0=op0, op1=op1), reads, writes)
            else:
                P.op("dve", lambda e: e.scalar_tensor_tensor(out=out, in0=in0, scalar=scalar, in1=in1, op0=op0, op1=op1,
                                                             accum_out=accum_out), reads, writes)

        def CP(eng, out, in_, reads, writes):
            if eng == "act":
                ACT(out, in_, AF.Copy, reads, writes)
            else:
                P.op(eng, lambda e: e.tensor_copy(out=out, in_=in_), reads, writes)

        def RCP(out, in_, reads, writes):
            P.op("dve", lambda e: e.reciprocal(out=out, in_=in_), reads, writes)

        def MSET(eng, ap, val, writes):
            P.op(eng, lambda e: e.memset(ap, val), (), writes)

        def DMA(q, out, in_, slot, reads, writes):
            P.op(q, lambda e: e.dma_start(out=out, in_=in_), reads, writes, slot=slot)

        hT = A.get([128, 8, S], BF16)
        GB = A.get([128, D], F32)
        ident = A.get([128, 128], BF16)
        ones_bf = A.get([128, 64], BF16)
        ones_f = A.get([1, 128], F32)
        eps_t = A.get([128, 1], F32)
        cT = A.get([128, 8], F32)
        ng = A.get([128, 8], F32)
        gs = A.get([128, 8], F32)
        shc = A.get([128, 8], F32)
        ss = A.get([128, NT], F32)
        sd = A.get([128, NT], F32)
        rstd = A.get([128, NT], F32)
        persist_top = A.top

        DMA("pool", ident, ident_d, "c_ident", (), ["ident"])
        MSET("pool", ones_bf, 1.0, ["ones_bf"])
        MSET("pool", ones_f, 1.0, ["ones_f"])
        MSET("pool", eps_t, EPS, ["eps"])
        DMA("sp", cT, cT_d, "c_cT", (), ["cT"])
        DMA("sp", ng, ng_d, "c_ng", (), ["ng"])

        wa = [A.get([128, 2 * D], F32) for _ in range(3)]
        bada = A.get([1, 3 * D], F32)
        mod = A.get([1, 3 * D], F32)
        wu = [A.get_top([128, 3, 8, 128], BF16) for _ in range(2)]
        wzb = [A.get_top([128, 8, 128], BF16) for _ in range(2)]
        Etab = [A.get_top([128, 3, 512], BF16) for _ in range(2)]
        bst = A.get_top([128, 512], F32)
        xt = [A.get([128, D], F32) for _ in range(8)]
        xh = [A.get([128, D], BF16) for _ in range(8)]
        junk = A.get([128, D], BF16)

        DMA("sp", bada, bada_d, "c_bada", (), ["bada"])
        def load_wa(kc):
            DMA("sp", wa[kc % 3], wada_d[kc * 128:(kc + 1) * 128, 0:2 * D], f"wa{kc % 3}", (), [("wa", kc % 3)])

        def load_xt(t):
            DMA("sp", xt[t % 8], x[t * 128:(t + 1) * 128, :], f"xt{t % 8}", (), [("xt", t % 8)])

        for kc in range(3):
            load_wa(kc)
        for t in range(8):
            load_xt(t)
        for kc in range(8):
            if kc >= 3:
                load_wa(kc)
            for n in range(4):
                MM(bank(n)[0:1, :], cT[:, kc:kc + 1], wa[kc % 3][:, n * 512:(n + 1) * 512], kc == 0, kc == 7,
                   [("wa", kc % 3), "cT"], [("ps", n)])
        DMA("sp", GB[0:1, :], bada_d[0:1, 2 * D:3 * D], "c_gb0", (), ["GBrow"])
        for n in range(4):
            TT("dve", mod[0:1, n * 512:(n + 1) * 512], bank(n)[0:1, :], bada[0:1, n * 512:(n + 1) * 512], ALU.add,
               [("ps", n), "bada"], [("mod", n)])
        for j in range(16):
            MM(bank(6)[:, j:j + 1], mod[0:1, j * 128:(j + 1) * 128], ones_f[0:1, 0:1], True, True,
               [("mod", j // 4), "ones_f"], [("ps", 6)])
        CP("dve", shc, bank(6)[:, 0:8], [("ps", 6)], ["shc"])
        STT(gs, bank(6)[:, 8:16], 1.0, ng, ALU.add, ALU.mult, [("ps", 6), "ng"], ["gs"])

        def load_unit_weights(u):
            hp_, gi_ = divmod(u, 3)
            par_ = u % 2
            for j in range(3):
                c0 = gi_ * 1536 + j * 512 + hp_ * 128
                DMA("pool", wu[par_][:, j, :, :], win_v[:, :, c0:c0 + 128], f"wu{par_}{j}", (), [("wu", par_, j)])

        def load_hp_tables(hp_):
            for gi_ in range(3):
                DMA("pool", bst, bg_d[:, hp_ * 3 + gi_, :], "bst", (), ["bst"])
                ACT(Etab[hp_ % 2][:, gi_, :], bst, AF.Exp, ["bst"], [("E", hp_ % 2, gi_)])
            c0z = 4608 + hp_ * 128
            DMA("pool", wzb[hp_ % 2], win_v[:, :, c0z:c0z + 128], f"wz{hp_ % 2}", (), [("wz", hp_ % 2)])

        load_unit_weights(0)
        load_hp_tables(0)

        trbanks = (7, 5)
        tri = [0]

        def stageA(grp):
            for i in range(4):
                t = grp * 4 + i
                sl = t % 8
                if t >= 8:
                    load_xt(t)
                STT(junk, xt[sl], 1.0, xt[sl], ALU.mult, ALU.mult, [("xt", sl)], ["junk", ("ss", t)],
                    accum_out=ss[:, t:t + 1])
            g4 = slice(grp * 4, grp * 4 + 4)
            ACT(sd[:, g4], ss[:, g4], AF.Sqrt, [("ss", grp * 4 + i) for i in range(4)] + ["eps"], [("sd", grp)],
                scale=1.0 / D, bias=eps_t[:, 0:1])
            RCP(rstd[:, g4], sd[:, g4], [("sd", grp)], [("rstd", grp)])
            for i in range(4):
                t = grp * 4 + i
                sl = t % 8
                xi = (grp % 2) * 4 + i
                if i < 2:
                    ACT(xh[xi], xt[sl], AF.Identity, [("xt", sl), ("rstd", grp)], [("xh", xi)], scale=rstd[:, t:t + 1])
                else:
                    TS("dve", xh[xi], xt[sl], rstd[:, t:t + 1], None, ALU.mult, None,
                       [("xt", sl), ("rstd", grp)], [("xh", xi)])

        def stageB(grp):
            for k2 in range(4):
                b = trbanks[tri[0] % 2]
                tri[0] += 1
                trb = bank_bf(b)
                for kcl in range(2):
                    kc = k2 * 2 + kcl
                    for i in range(4):
                        xi = (grp % 2) * 4 + i
                        TR(trb[:, kcl * 512 + i * 128:kcl * 512 + (i + 1) * 128], xh[xi][:, kc * 128:(kc + 1) * 128], ident,
                           [("xh", xi), "ident"], [("ps", b)])
                for kcl in range(2):
                    kc = k2 * 2 + kcl
                    if k2 < 3:
                        ACT(hT[:, kc, grp * 512:(grp + 1) * 512], trb[:, kcl * 512:(kcl + 1) * 512], AF.Identity,
                            [("ps", b), "gs", "shc"], [("hT", kc, grp)], scale=gs[:, kc:kc + 1], bias=shc[:, kc:kc + 1])
                    else:
                        TS("dve", hT[:, kc, grp * 512:(grp + 1) * 512], trb[:, kcl * 512:(kcl + 1) * 512],
                           gs[:, kc:kc + 1], shc[:, kc:kc + 1], ALU.mult, ALU.add,
                           [("ps", b), "gs", "shc"], [("hT", kc, grp)])

        stageA(0)
        for grp in range(8):
            if grp + 1 < 8:
                stageA(grp + 1)
            stageB(grp)

        def dump(src2d, name):
            P.barrier()
            DMA("sp", dbg_d, src2d, "dbg", [name], ["dbgout"])
            P.op("sp", None, ["dbgout"], ())

        def finish():
            P.finalize()
            sems = {"dma": {}, "eng": {}}
            for slot in P.slotcnt:
                sems["dma"][slot] = es.enter_context(nc.semaphore("d_" + str(slot)))
            for e in P.ENG:
                nep = max(1, (P.nsig[e] + EPOCH - 1) // EPOCH)
                sems["eng"][e] = [es.enter_context(nc.semaphore(f"e_{e}{i}")) for i in range(nep)]
            block = es.enter_context(nc.Block())

            @block.tensor
            def _(eng):
                P.emit("pe", eng, sems)

            @block.scalar
            def _(eng):
                P.emit("act", eng, sems)

            @block.vector
            def _(eng):
                P.emit("dve", eng, sems)

            @block.gpsimd
            def _(eng):
                P.emit("pool", eng, sems)

            @block.sync
            def _(eng):
                P.emit("sp", eng, sems)

        if dbg == "hT":
            P.lastw["hTall"] = None
            dump(hT.rearrange("p a b -> p (a b)"), "hTall")
            finish()
            return nc

        P.barrier()
        A.top = persist_top
        AT = A.get([128, 4, S], BF16)
        p2_top = A.top
        QT = A.get([128, S], BF16)
        KT = A.get([128, S], BF16)
        Vaug = A.get([128, 32, 2, 128], BF16)
        VT = A.get([128, 2048], BF16)
        acc = A.get([128, 2, S], F32)
        Xe = [A.get([128, 2, 256], BF16) for _ in range(2)]
        PT = [A.get([128, 2, 256], BF16) for _ in range(4)]
        zs = A.get([128, 512], F32)
        rec = [A.get([128, 512], F32) for _ in range(2)]
        wagp = A.get([128, 512], F32)

        def gate_load(idx):
            kc_, n_ = divmod(idx, 2)
            DMA("sp", wagp, wada_d[kc_ * 128:(kc_ + 1) * 128, 2 * D + n_ * 512:2 * D + (n_ + 1) * 512], "wagp", (), ["wagp"])

        def gate_piece(idx):
            kc_, n_ = divmod(idx, 2)
            b_ = next_proj_bank()
            MM(bank(b_)[0:1, :], cT[:, kc_:kc_ + 1], wagp, True, True, ["wagp", "cT"], [("ps", b_)])
            TT("dve", GB[0:1, n_ * 512:(n_ + 1) * 512], bank(b_)[0:1, :], GB[0:1, n_ * 512:(n_ + 1) * 512], ALU.add,
               [("ps", b_), "GBrow"], ["GBrow"])

        def gate_broadcast():
            for n_ in range(2):
                b_ = next_proj_bank()
                MM(bank(b_), ones_f[0:1, 0:128], GB[0:1, n_ * 512:(n_ + 1) * 512], True, True, ["GBrow", "ones_f"], [("ps", b_)])
                CP("dve", GB[:, n_ * 512:(n_ + 1) * 512], bank(b_), [("ps", b_)], [("GB", n_), "GBrow"])

        MSET("pool", Vaug.rearrange("p a b c -> p (a b c)"), 1.0, ["Vones"] + [(("V", k_), h_) for k_ in range(32) for h_ in range(2)])

        PROJ_BANKS = (0, 1, 6, 7)
        STA = (0, 1)
        STB = (2, 3)
        NDB = (4, 5, 6, 7)
        TRB = (2, 3, 4, 5)
        pj = [0]

        def next_proj_bank():
            b = PROJ_BANKS[pj[0] % len(PROJ_BANKS)]
            pj[0] += 1
            return b

        evi = [0]

        def evac_engine():
            evi[0] += 1
            return "act" if evi[0] % 2 == 0 else "dve"

        def finish_slice(pf, s):
            fhp, akeys, fkeys = pf
            wz = wzb[fhp % 2]
            b = next_proj_bank()
            for kc in range(8):
                MM(bank(b), wz[:, kc, :], hT[:, kc, s * 512:(s + 1) * 512], kc == 0, kc == 7,
                   [("wz", fhp % 2), ("hT", kc, s)], [("ps", b)])
            ACT(zs, bank(b), AF.Silu, [("ps", b)], ["zs"])
            sl = slice(s * 512, (s + 1) * 512)
            rc = rec[s % 2]
            ra = ("rec", s % 2, 0)
            rb = ("rec", s % 2, 1)
            DMA("sp", rc[0:64, :], acc[64:128, 0, sl], f"rcA{s % 2}", akeys, [ra, ("accF", fhp, s, 0)])
            DMA("sp", rc[64:128, :], acc[0:64, 1, sl], f"rcB{s % 2}", akeys, [rb, ("accF", fhp, s, 1)])
            RCP(rc, rc, [ra, rb], [ra, rb])
            TT("dve", rc[0:64, :], rc[0:64, :], acc[0:64, 0, sl], ALU.mult, [ra] + akeys, [ra, ("accF", fhp, s, 2)])
            TT("dve", rc[64:128, :], rc[64:128, :], acc[64:128, 1, sl], ALU.mult, [rb] + akeys, [rb, ("accF", fhp, s, 3)])
            TT("pool", AT[:, fhp, sl], rc, zs, ALU.mult, [ra, rb, "zs"], [("AT", fhp)])
            fkeys += [("accF", fhp, s, k) for k in range(4)]

        hps = list(range(4)) if dbg != "AT0" else [0]
        acc_prev = []
        pend_fin = None
        for hp in hps:
            epar = hp % 2
            if hp > 0:
                load_hp_tables(hp)
            for gi in range(3):
                u = hp * 3 + gi
                par = u % 2
                dil = DILS[gi]
                nb = 32 // dil
                Ls = 512 // dil
                if u + 1 < 12 and not (dbg == "AT0" and u + 1 >= 3):
                    load_unit_weights(u + 1)
                if 1 <= u <= 8:
                    gate_load(2 * (u - 1))
                if u == 9:
                    gate_broadcast()
                for s in range(NS):
                    if pend_fin is not None and gi == 0:
                        finish_slice(pend_fin, s)
                    if 1 <= u <= 8 and s == 3:
                        gate_piece(2 * (u - 1))
                        gate_load(2 * (u - 1) + 1)
                    if 1 <= u <= 8 and s == 7:
                        gate_piece(2 * (u - 1) + 1)
                    for j in range(3):
                        b = next_proj_bank()
                        for kc in range(8):
                            MM(bank(b), wu[par][:, j, kc, :], hT[:, kc, s * 512:(s + 1) * 512], kc == 0, kc == 7,
                               [("wu", par, j), ("hT", kc, s)], [("ps", b)])
                        if dil == 1:
                            src = bank(b)
                        else:
                            src = bank(b).rearrange("p (l r) -> p r l", r=dil)
                        if j < 2:
                            T = QT if j == 0 else KT
                            if dil == 1:
                                dst = T[:, s * 512:(s + 1) * 512]
                            else:
                                dst = T.rearrange("p (r l) -> p r l", r=dil)[:, :, s * Ls:(s + 1) * Ls]
                            CP(evac_engine(), dst, src, [("ps", b)], [("QT" if j == 0 else "KT", s)])
                        else:
                            if dil == 1:
                                dst = VT[:, (s % 4) * 512:(s % 4 + 1) * 512]
                            else:
                                dst = VT.rearrange("p (r l) -> p r l", r=dil)[:, :, (s % 4) * Ls:(s % 4 + 1) * Ls]
                            CP(evac_engine(), dst, src, [("ps", b)], [("VT", s % 4)])
                            Lv = 2048 // dil
                            if dil == 1:
                                blks = [[(0, 4 * s + i) for i in range(4)]]
                            elif dil == 4:
                                blks = [[(r, s) for r in range(4)]]
                            else:
                                blks = [[(r0 + i, s // 4) for i in range(4)] for r0 in range(0, 16, 4)] if s % 4 == 3 else []
                            for bl in blks:
                                tb = TRB[(s + bl[0][0] // 4) % 4]
                                trb = bank_bf(tb)
                                for i, (r, n) in enumerate(bl):
                                    off = r * Lv + (n * 128) % Lv
                                    TR(trb[:, i * 128:(i + 1) * 128], VT[:, off:off + 128], ident,
                                       [("VT", q) for q in range(4)] + ["ident"], [("ps", tb)])
                                kb0 = bl[0][0] * nb + bl[0][1]
                                kstep = (bl[1][0] * nb + bl[1][1]) - kb0
                                kbs = slice(kb0, kb0 + 3 * kstep + 1, kstep)
                                srcv = trb[:, 0:512].rearrange("p (a b) -> p a b", a=4)
                                vkeys = [("V", bl[i][0] * nb + bl[i][1]) for i in range(4)]
                                ve = evac_engine()
                                CP(ve, Vaug[:, kbs, 0, 0:64], srcv[:, :, 0:64], [("ps", tb)], [(k, 0) for k in vkeys])
                                CP(ve, Vaug[:, kbs, 1, 64:128], srcv[:, :, 64:128], [("ps", tb)], [(k, 1) for k in vkeys])
                if pend_fin is not None and gi == 0:
                    acc_prev = pend_fin[2]
                    pend_fin = None
                blocks = [(r, n) for r in range(dil) for n in range(nb)]
                NBk = len(blocks)
                qk_reads = [("QT", s_) for s_ in range(NS)] + [("KT", s_) for s_ in range(NS)]
                ev = Etab[epar][:, gi, :].rearrange("p (h c) -> p h c", h=2)
                accv = acc.rearrange("p c (l r) -> p c r l", r=dil)
                cur_keys = []

                def Nof(i):
                    return 256 if blocks[i][1] < nb - 1 else 128

                def S_(i):
                    N = Nof(i)
                    for hb, banks in ((0, STA), (1, STB)):
                        b = banks[i % 2]
                        rows = slice(hb * 64, (hb + 1) * 64)
                        MM(bank(b)[:, 0:N], KT[rows, i * 128:(i + 1) * 128], QT[rows, i * 128:i * 128 + N],
                           True, True, qk_reads, [("ps", b)])

                def X_(i):
                    N = Nof(i)
                    for hb, banks in ((0, STA), (1, STB)):
                        b = banks[i % 2]
                        ACT(Xe[i % 2][:, hb, 0:N], bank(b)[:, 0:N], AF.Exp, [("ps", b)], [("Xe", i % 2, hb)], scale=0.125)

                def M_(i):
                    N = Nof(i)
                    TT("dve", PT[i % 4][:, :, 0:N], Xe[i % 2][:, :, 0:N], ev[:, :, 0:N], ALU.mult,
                       [("Xe", i % 2, 0), ("Xe", i % 2, 1), ("E", epar, gi)], [("PT", i % 4)])

                def V_(i):
                    r, n = blocks[i]
                    nd = NDB[i % 4]
                    for hb in range(2):
                        cols = slice(hb * 128, (hb + 1) * 128)
                        if n > 0:
                            MM(bank(nd)[:, cols], Vaug[:, i - 1, hb, :], PT[(i - 1) % 4][:, hb, 128:256], True, False,
                               [(("V", i - 1), hb), "Vones", ("PT", (i - 1) % 4)], [("ps", nd)])
                        MM(bank(nd)[:, cols], Vaug[:, i, hb, :], PT[i % 4][:, hb, 0:128], n == 0, True,
                           [(("V", i), hb), "Vones", ("PT", i % 4)], [("ps", nd)])

                def A_(i):
                    r, n = blocks[i]
                    nd = NDB[i % 4]
                    dstv = accv[:, :, r, n * 128:(n + 1) * 128]
                    srcv = bank(nd)[:, 0:256].rearrange("p (c q) -> p c q", c=2)
                    key = ("accA", u, i)
                    cur_keys.append(key)
                    if gi == 0:
                        CP("act", dstv, srcv, [("ps", nd)] + acc_prev, [key])
                    else:
                        TT("dve", dstv, srcv, dstv, ALU.add, [("ps", nd)] + acc_prev, [key])

                NBe = NBk
                for step in range(NBe + 4):
                    if step < NBe:
                        S_(step)
                        X_(step)
                        M_(step)
                    if 0 <= step - 2 < NBe:
                        V_(step - 2)
                    if 0 <= step - 4 < NBe:
                        A_(step - 4)
                acc_prev = cur_keys

            if dbg == "acc":
                dump(acc.rearrange("p a b -> p (a b)"), "acc")
                finish()
                return nc
            pend_fin = (hp, list(acc_prev), [])
            if hp == hps[-1]:
                for s_ in range(NS):
                    finish_slice(pend_fin, s_)
                acc_prev = pend_fin[2]
                pend_fin = None

        if dbg in ("AT", "AT0"):
            P.lastw["ATall"] = None
            dump(AT.rearrange("p a b -> p (a b)"), "ATall")
            finish()
            return nc

        P.barrier()
        A.top = p2_top
        A.hi = A.n
        M2T = A.get([128, 4, S], BF16)
        p3_top = A.top
        Up = A.get([128, 16 + S], F32)
        Ta = A.get([128, 16 + S], F32)
        Tb = A.get([128, 16 + S], F32)
        pooled = A.get([128, S], BF16)
        wu2 = [A.get([128, 2, 8, 128], BF16) for _ in range(2)]
        pw = A.get([128, 4, 128], BF16)
        pscale = A.get([128, 4], F32)
        icnt = A.get([128, 64], F32)
        zs3 = [A.get([128, 512], BF16) for _ in range(4)]
        t16 = A.get([128, 16], F32)

        DMA("sp", pscale, pscale_d, "c_pscale", (), ["pscale"])
        DMA("sp", icnt, icnt_d, "c_icnt", (), ["icnt"])
        DMA("pool", pw, poolw_d.rearrange("g c e -> c g e"), "c_pw", (), ["pw"])
        MSET("pool", Up[:, 0:16], 0.0, ["pad"])
        MSET("pool", Ta[:, 0:16], 0.0, ["pad"])
        MSET("pool", Tb[:, 0:16], 0.0, ["pad"])

        def load_pool_weights(pg):
            for j, base in enumerate((5120, 5632)):
                c0 = base + pg * 128
                DMA("pool", wu2[pg % 2][:, j, :, :], win_v[:, :, c0:c0 + 128], f"wu2{pg % 2}{j}", (), [("wu2", pg % 2, j)])

        load_pool_weights(0)
        LA = 3

        def uproj(pg, s):
            w2 = wu2[pg % 2]
            b = next_proj_bank()
            for kc in range(8):
                MM(bank(b), w2[:, 0, kc, :], hT[:, kc, s * 512:(s + 1) * 512], kc == 0, kc == 7,
                   [("wu2", pg % 2, 0), ("hT", kc, s)], [("ps", b)])
            CP(evac_engine(), Up[:, 16 + s * 512:16 + (s + 1) * 512], bank(b), [("ps", b)], [("Up", s)])

        for s in range(NS):
            uproj(0, s)
        upk = [("Up", s_) for s_ in range(NS)]
        for pg in range(4):
            if pg + 1 < 4:
                load_pool_weights(pg + 1)
            w2 = wu2[pg % 2]
            win = 2 ** (pg + 1)
            a, akeys = Up, upk
            for k in range(pg + 1):
                bbuf, bname = (Ta, "Ta") if k % 2 == 0 else (Tb, "Tb")
                sh = 2 ** k
                TT("dve", bbuf[:, 16:16 + S], a[:, 16:16 + S], a[:, 16 - sh:16 - sh + S], ALU.add, akeys + ["pad"], [bname])
                a, akeys = bbuf, [bname]
            STT(pooled, a[:, 16:16 + S], 1.0 / win, Up[:, 16:16 + S], ALU.mult, ALU.subtract, akeys + upk, ["pooled"])
            TT("dve", t16, a[:, 16:32], icnt[:, pg * 16:(pg + 1) * 16], ALU.mult, akeys + ["icnt"], ["t16"])
            TT("dve", pooled[:, 0:16], t16, Up[:, 16:32], ALU.subtract, ["t16", "pooled"] + upk, ["pooled"])
            for step in range(NS + LA):
                if step < NS:
                    s = step
                    sl = slice(s * 512, (s + 1) * 512)
                    b2 = 2 + s % 4
                    for kc in range(8):
                        MM(bank(b2), w2[:, 1, kc, :], hT[:, kc, sl], kc == 0, kc == 7,
                           [("wu2", pg % 2, 1), ("hT", kc, s)], [("ps", b2)])
                    ACT(zs3[s % 4], bank(b2), AF.Silu, [("ps", b2)], [("zs3", s % 4)])
                if step >= LA:
                    s = step - LA
                    sl = slice(s * 512, (s + 1) * 512)
                    b1 = next_proj_bank()
                    MM(bank(b1), pw[:, pg, :], pooled[:, sl], True, True, ["pw", "pooled"], [("ps", b1)])
                    STT(M2T[:, pg, sl], bank(b1), pscale[:, pg:pg + 1], zs3[s % 4], ALU.mult, ALU.mult,
                        [("ps", b1), ("zs3", s % 4), "pscale"], [("M2T", pg)])
                    if pg + 1 < 4:
                        uproj(pg + 1, s)

        if dbg == "M2T":
            P.lastw["M2Tall"] = None
            dump(M2T.rearrange("p a b -> p (a b)"), "M2Tall")
            finish()
            return nc

        P.barrier()
        A.top = p3_top
        wg = A.get([128, 2, 8, D], BF16)
        wab = A.get([128, 4, D], BF16)
        wpb = A.get([128, 4, D], BF16)
        p4_top = A.top
        sg = A.get([128, 2, 8, 512], BF16)
        t1 = [A.get([128, 512], F32) for _ in range(2)]
        t2 = [A.get([128, 512], F32) for _ in range(2)]

        for fb in range(4):
            for wsel in range(2):
                c0g = 6144 + wsel * 1024 + fb * 256
                DMA("pool", wg[:, wsel, :, fb * 256:(fb + 1) * 256], win_v[:, :, c0g:c0g + 256],
                    f"wg{wsel}{fb}", (), [("wg", wsel, fb)])
        DMA("pool", wab, wab_d.rearrange("(c p) f -> p c f", p=128), "c_wab", (), ["wab"])
        DMA("pool", wpb, wpb_d.rearrange("(c p) f -> p c f", p=128), "c_wpb", (), ["wpb"])

        G_BANKS = (0, 1, 2, 3)
        Y_BANKS = (4, 5, 6, 7)
        gi_ = 0
        yi_ = 0
        for s in range(NS):
            sl = slice(s * 512, (s + 1) * 512)
            for f in range(8):
                for wsel in range(2):
                    b = G_BANKS[gi_ % 4]
                    gi_ += 1
                    for kc in range(8):
                        MM(bank(b), wg[:, wsel, kc, f * 128:(f + 1) * 128], hT[:, kc, sl], kc == 0, kc == 7,
                           [("wg", wsel, f // 2), ("hT", kc, s)], [("ps", b)])
                    ACT(sg[:, wsel, f, :], bank(b), AF.Sigmoid, [("ps", b)], [("sg", wsel, f)])
            for f in range(8):
                ba = Y_BANKS[yi_ % 4]
                bp = Y_BANKS[(yi_ + 1) % 4]
                yi_ += 2
                for c in range(4):
                    MM(bank(ba), wab[:, c, f * 128:(f + 1) * 128], AT[:, c, sl], c == 0, c == 3,
                       ["wab", ("AT", c)], [("ps", ba)])
                for c in range(4):
                    MM(bank(bp), wpb[:, c, f * 128:(f + 1) * 128], M2T[:, c, sl], c == 0, c == 3,
                       ["wpb", ("M2T", c)], [("ps", bp)])
                TT("dve", t1[f % 2], bank(ba), sg[:, 0, f, :], ALU.mult, [("ps", ba), ("sg", 0, f)], [("t1", f % 2)])
                TT("dve", t2[f % 2], bank(bp), sg[:, 1, f, :], ALU.mult, [("ps", bp), ("sg", 1, f)], [("t2", f % 2)])
                TT("pool", hT[:, f, sl], t1[f % 2], t2[f % 2], ALU.add, [("t1", f % 2), ("t2", f % 2)], [("hT", f, s)])

        if dbg == "mg":
            P.lastw["hTall"] = None
            dump(hT.rearrange("p a b -> p (a b)"), "hTall")
            finish()
            return nc

        P.barrier()
        A.top = p2_top
        wout = A.get([128, 8, D], BF16)
        FG = A.get([128, D], F32)
        x2 = [A.get([128, D], F32) for _ in range(4)]
        xn = [A.get([128, D], F32) for _ in range(2)]
        yt = [A.get([128, D], F32) for _ in range(3)]
        junk2 = A.get([128, D], BF16)
        ss2 = A.get([128, NT], F32)
        sd2 = A.get([128, NT], F32)
        r2 = A.get([128, NT], F32)

        for m in range(8):
            DMA("pool", wout[:, m, :], wout_d[m * 128:(m + 1) * 128, :], f"c_wout{m}", (), [("wout", m)])
        for m in range(8):
            TT("dve", wout[:, m, :], wout[:, m, :], GB, ALU.mult, [("wout", m), ("GB", 0), ("GB", 1)], [("wout", m)])
        DMA("sp", FG, fg_d, "c_fg", (), ["FG"])
        outs = []
        OB = ((0, 1), (2, 3), (4, 5), (6, 7))
        def load_x2(t_):
            DMA("sp", x2[t_ % 4], x[t_ * 128:(t_ + 1) * 128, :], f"x2{t_ % 4}", (), [("x2", t_ % 4)])

        for t_ in range(3):
            load_x2(t_)
        for tt in range(NT):
            s = tt // 4
            xs = tt % 4
            if tt + 3 < NT:
                load_x2(tt + 3)
            xb = xn[tt % 2]
            for half in range(2):
                b = OB[tt % 4][half]
                for m in range(8):
                    MM(bank(b), hT[:, m, tt * 128:(tt + 1) * 128], wout[:, m, half * 512:(half + 1) * 512], m == 0, m == 7,
                       [("hT", m, s), ("wout", m)], [("ps", b)])
                TT("dve", xb[:, half * 512:(half + 1) * 512], bank(b), x2[xs][:, half * 512:(half + 1) * 512], ALU.add,
                   [("ps", b), ("x2", xs)], [("xn", tt % 2, half)])
            ACT(junk2, xb, AF.Square, [("xn", tt % 2, 0), ("xn", tt % 2, 1)], ["junk2", ("ss2", tt)], accum_out=ss2[:, tt:tt + 1])
            ACT(sd2[:, tt:tt + 1], ss2[:, tt:tt + 1], AF.Sqrt, [("ss2", tt), "eps"], [("sd2", tt)], scale=1.0 / D, bias=eps_t[:, 0:1])
            RCP(r2[:, tt:tt + 1], sd2[:, tt:tt + 1], [("sd2", tt)], [("r2", tt)])
            ys = tt % 3
            STT(yt[ys], xb, r2[:, tt:tt + 1], FG, ALU.mult, ALU.mult, [("xn", tt % 2, 0), ("xn", tt % 2, 1), ("r2", tt), "FG"], [("yt", ys)])
            DMA("sp", y[tt * 128:(tt + 1) * 128, :], yt[ys], f"yt{ys}", [("yt", ys)], [("yout", tt)])
        P.op("sp", None, [("yout", tt) for tt in range(NT)], ())
        finish()
    return nc


def _t5_bucket(n):
    max_exact = 16
    nf = np.maximum(n, 1).astype(np.float32)
    large = max_exact + (np.log(nf / np.float32(max_exact)) / np.float32(math.log(2048 / max_exact))
                         * np.float32(32 - max_exact)).astype(np.int32)
    large = np.minimum(large, 31)
    return np.where(n < max_exact, n, large)


def _const_tables(rel_bias):
    k = np.arange(128)[:, None]
    q = np.arange(128)[None, :]
    dist_prev = np.clip(128 + q - k, 0, 128)
    dist_cur = np.clip(q - k, 0, 128)
    ok_prev = np.broadcast_to(k >= q, (128, 128))
    ok_cur = np.broadcast_to(k <= q, (128, 128))
    bg = np.full((128, 12, 2, 2, 128), -1.0e4, np.float32)
    for gi, dil in enumerate(DILS):
        bp = _t5_bucket(dist_prev * dil)
        bc = _t5_bucket(dist_cur * dil)
        for hp in range(4):
            for hb in range(2):
                col = gi * 8 + hp * 2 + hb
                gc = rel_bias[bc, col]
                gp = rel_bias[bp, col]
                bg[:, hp * 3 + gi, hb, 0, :][ok_cur] = gc[ok_cur]
                bg[:, hp * 3 + gi, hb, 1, :][ok_prev] = gp[ok_prev]
    icnt = np.zeros((128, 4, 16), np.float32)
    for pg in range(4):
        win = 2 ** (pg + 1)
        icnt[:, pg, :] = 1.0 / np.minimum(np.arange(16) + 1, win).astype(np.float32)
    return bg.reshape(128, 12, 512), icnt.reshape(128, 64)


def make_in_maps(inputs):
    f = lambda a: np.ascontiguousarray(np.asarray(a, dtype=np.float32))
    x = f(inputs["x"])
    c = f(inputs["c"])
    bg, icnt = _const_tables(f(inputs["rel_bias"]))
    shared = {
        "ng": f(f(inputs["norm_g"])[0].reshape(8, 128).T),
        "w_ada": f(inputs["w_ada"])[0],
        "b_ada": f(inputs["b_ada"])[0].reshape(1, 3 * D),
        "w_in": f(inputs["w_in"])[0],
        "pool_w": f(inputs["pool_w"])[0],
        "pscale": f(f(inputs["pool_scale"])[0].reshape(4, 128).T),
        "w_ab": f(inputs["w_attn_br"])[0],
        "w_pb": f(inputs["w_pool_br"])[0],
        "w_out": f(inputs["w_out"])[0],
        "bg": bg, "icnt": icnt,
        "ident": np.eye(128, dtype=np.float32),
        "fg": f(np.broadcast_to(f(inputs["final_g"])[None, :], (128, D))),
    }
    maps = []
    for b in range(x.shape[0]):
        m = dict(shared)
        m["x"] = x[b]
        m["cT"] = f(c[b].reshape(8, 128).T)
        maps.append(m)
    return maps


_NC_CACHE = {}


def kernel(**inputs):
    maps = make_in_maps(inputs)
    if "nc" not in _NC_CACHE:
        _NC_CACHE["nc"] = build()
    res = run_bass_kernel_spmd(_NC_CACHE["nc"], maps, core_ids=list(range(8)))
    return np.stack([np.asarray(r["y"], dtype=np.float32) for r in res.results], axis=0)
```

```python
import math
from contextlib import ExitStack

import numpy as np
import concourse.bass as bass
import concourse.mybir as mybir
from concourse.bass_utils import run_bass_kernel_spmd

F32 = mybir.dt.float32
BF16 = mybir.dt.bfloat16
AF = mybir.ActivationFunctionType
ALU = mybir.AluOpType

S = 4096
D = 1024
NT = 32
NS = 8
EPS = 1e-6
DILS = (1, 4, 16)
EPOCH = 4000
SBUF_WORDS = 53200


class Op:
    __slots__ = ("eng", "fn", "slot", "slotn", "signal", "count", "deps", "waits")


class Prog:
    ENG = ("pe", "act", "dve", "pool", "sp")

    def __init__(self):
        self.ops = {e: [] for e in self.ENG}
        self.lastw = {}
        self.readers = {}
        self.slotcnt = {}
        self.lastdma = {}
        self.lastcomp = {}
        self.pending = {e: set() for e in self.ENG}

    def op(self, eng, fn, reads=(), writes=(), slot=None):
        o = Op()
        o.eng = eng
        o.fn = fn
        o.slot = slot
        o.signal = False
        o.count = 0
        o.slotn = 0
        deps = set(self.pending[eng])
        self.pending[eng] = set()
        for k in reads:
            w = self.lastw.get(k)
            if w is not None:
                deps.add(w)
        for k in writes:
            w = self.lastw.get(k)
            if w is not None:
                deps.add(w)
            rd = self.readers.get(k)
            if rd:
                for r in rd.values():
                    if isinstance(r, list):
                        deps.update(r)
                    else:
                        deps.add(r)
        for k in reads:
            rd = self.readers.setdefault(k, {})
            if slot is not None:
                rd.setdefault("_dma", []).append(o)
            else:
                rd[eng] = o
        for k in writes:
            self.lastw[k] = o
            self.readers[k] = {}
        deps.discard(o)
        o.deps = deps
        if slot is not None:
            n = self.slotcnt.get(slot, 0) + 1
            self.slotcnt[slot] = n
            o.slotn = n
            self.lastdma[slot] = o
        else:
            self.lastcomp[eng] = o
        self.ops[eng].append(o)
        return o

    def barrier(self):
        allops = set(self.lastcomp.values()) | set(self.lastdma.values())
        for e in self.ENG:
            self.pending[e] |= allops

    def finalize(self):
        for e in self.ENG:
            for o in self.ops[e]:
                for d in o.deps:
                    if d.slot is None and not (d.eng == "pe" and e == "pe"):
                        d.signal = True
        self.nsig = {}
        for e in self.ENG:
            c = 0
            for o in self.ops[e]:
                if o.slot is None and o.signal:
                    c += 1
                    o.count = c
            self.nsig[e] = c
        for e in self.ENG:
            seen = {}
            for o in self.ops[e]:
                w = {}
                for d in o.deps:
                    if d.slot is not None:
                        key = ("dma", d.slot)
                        val = d.slotn
                    else:
                        if d.eng == "pe" and e == "pe":
                            continue
                        key = ("eng", d.eng)
                        val = d.count
                    if seen.get(key, 0) >= val:
                        continue
                    if w.get(key, 0) < val:
                        w[key] = val
                for k, v in w.items():
                    seen[k] = v
                o.waits = w

    def emit(self, e, eng, sems):
        for o in self.ops[e]:
            for (kind, name), v in o.waits.items():
                if kind == "dma":
                    eng.wait_ge(sems["dma"][name], 16 * v)
                else:
                    ep = (v - 1) // EPOCH
                    eng.wait_ge(sems["eng"][name][ep], v - ep * EPOCH)
            if o.fn is None:
                continue
            inst = o.fn(eng)
            if o.slot is not None:
                inst.then_inc(sems["dma"][o.slot], 16)
            elif o.signal:
                ep = (o.count - 1) // EPOCH
                inst.then_inc(sems["eng"][e][ep], 1)


class Alloc:
    def __init__(self, big, nwords):
        self.big = big
        self.n = nwords
        self.top = 0
        self.hi = nwords

    def get_top(self, shape, dtype):
        save = self.top
        nel = 1
        for d in shape[1:]:
            nel *= d
        nw = ((nel * (4 if dtype == F32 else 2) + 3) // 4 + 7) // 8 * 8
        self.hi -= nw
        self.top = self.hi
        ap = self.get(shape, dtype, _chk=False)
        self.top = save
        assert self.top <= self.hi
        return ap

    def get(self, shape, dtype, _chk=True):
        nel = 1
        for d in shape[1:]:
            nel *= d
        nbytes = nel * (4 if dtype == F32 else 2)
        nw = (nbytes + 3) // 4
        off = self.top
        self.top += (nw + 7) // 8 * 8
        assert self.top <= (self.hi if _chk else self.n), f"SBUF overflow {self.top} > {self.hi}"
        ap = self.big[0:shape[0], off:off + nw]
        if dtype != F32:
            ap = ap.bitcast(dtype)
        if len(shape) > 2:
            names = "abcdef"[: len(shape) - 1]
            pat = "p (" + " ".join(names) + ") -> p " + " ".join(names)
            kw = {names[i]: shape[i + 1] for i in range(len(names))}
            ap = ap.rearrange(pat, **kw)
        return ap


def build(dbg=None):
    nc = bass.Bass("TRN2", target_bir_lowering=False)

    def din(name, shape):
        return nc.dram_tensor(name, shape, F32, kind="ExternalInput").ap()

    x = din("x", [S, D])
    cT_d = din("cT", [128, 8])
    ng_d = din("ng", [128, 8])
    wada_d = din("w_ada", [D, 3 * D])
    bada_d = din("b_ada", [1, 3 * D])
    win_d = din("w_in", [D, 8192])
    poolw_d = din("pool_w", [4, 128, 128])
    pscale_d = din("pscale", [128, 4])
    wab_d = din("w_ab", [512, D])
    wpb_d = din("w_pb", [512, D])
    wout_d = din("w_out", [D, D])
    bg_d = din("bg", [128, 12, 512])
    ident_d = din("ident", [128, 128])
    icnt_d = din("icnt", [128, 64])
    fg_d = din("fg", [128, D])
    y = nc.dram_tensor("y", [S, D], F32, kind="ExternalOutput").ap()
    dbg_d = None
    if dbg is not None:
        dshape = {"hT": ([128, 8 * S], BF16), "AT": ([128, 4 * S], BF16), "M2T": ([128, 4 * S], BF16),
                  "mg": ([128, 8 * S], BF16), "AT0": ([128, 4 * S], BF16), "acc": ([128, 2 * S], F32)}[dbg]
        dbg_d = nc.dram_tensor("dbg", dshape[0], dshape[1], kind="ExternalOutput").ap()

    win_v = win_d.rearrange("(kc p) f -> p kc f", p=128)

    P = Prog()
    es = ExitStack()
    with es:
        big = es.enter_context(nc.sbuf_tensor("big", [128, SBUF_WORDS], F32))
        psall = es.enter_context(nc.psum_tensor("psall", [128, 4096], F32))
        A = Alloc(big, SBUF_WORDS)

        def bank(b):
            return psall[:, b * 512:(b + 1) * 512]

        def bank_bf(b):
            return psall[:, b * 512:(b + 1) * 512].bitcast(BF16)

        def MM(out, lhsT, rhs, start, stop, reads, writes):
            P.op("pe", lambda e: e.matmul(out=out, lhsT=lhsT, rhs=rhs, start=start, stop=stop), reads, writes)

        def TR(out, in_, ident, reads, writes):
            P.op("pe", lambda e: e.transpose(out=out, in_=in_, identity=ident), reads, writes)

        def ACT(out, in_, func, reads, writes, scale=None, bias=None, accum_out=None):
            kw = {}
            if scale is not None:
                kw["scale"] = scale
            if bias is not None:
                kw["bias"] = bias
            if accum_out is not None:
                kw["accum_out"] = accum_out
            P.op("act", lambda e: e.activation(out=out, in_=in_, func=func, **kw), reads, writes)

        def TT(eng, out, in0, in1, op, reads, writes):
            P.op(eng, lambda e: e.tensor_tensor(out=out, in0=in0, in1=in1, op=op), reads, writes)

        def TS(eng, out, in0, s1, s2, op0, op1, reads, writes):
            if op1 is None:
                P.op(eng, lambda e: e.tensor_scalar(out=out, in0=in0, scalar1=s1, scalar2=None, op0=op0), reads, writes)
            else:
                P.op(eng, lambda e: e.tensor_scalar(out=out, in0=in0, scalar1=s1, scalar2=s2, op0=op0, op1=op1), reads, writes)

        def STT(out, in0, scalar, in1, op0, op1, reads, writes, accum_out=None):
            if accum_out is None:
                P.op("dve", lambda e: e.scalar_tensor_tensor(out=out, in0=in0, scalar=scalar, in1=in1, op0=op0, op1=op1), reads, writes)
            else:
                P.op("dve", lambda e: e.scalar_tensor_tensor(out=out, in0=in0, scalar=scalar, in1=in1, op0=op0, op1=op1,
                                                             accum_out=accum_out), reads, writes)

        def CP(eng, out, in_, reads, writes):
            if eng == "act":
                ACT(out, in_, AF.Copy, reads, writes)
            else:
                P.op(eng, lambda e: e.tensor_copy(out=out, in_=in_), reads, writes)

        def RCP(out, in_, reads, writes):
            P.op("dve", lambda e: e.reciprocal(out=out, in_=in_), reads, writes)

        def MSET(eng, ap, val, writes):
            P.op(eng, lambda e: e.memset(ap, val), (), writes)

        def DMA(q, out, in_, slot, reads, writes):
            P.op(q, lambda e: e.dma_start(out=out, in_=in_), reads, writes, slot=slot)

        hT = A.get([128, 8, S], BF16)
        GB = A.get([128, D], F32)
        ident = A.get([128, 128], BF16)
        ones_bf = A.get([128, 64], BF16)
        ones_f = A.get([1, 128], F32)
        eps_t = A.get([128, 1], F32)
        cT = A.get([128, 8], F32)
        ng = A.get([128, 8], F32)
        gs = A.get([128, 8], F32)
        shc = A.get([128, 8], F32)
        ss = A.get([128, NT], F32)
        sd = A.get([128, NT], F32)
        rstd = A.get([128, NT], F32)
        persist_top = A.top

        DMA("pool", ident, ident_d, "c_ident", (), ["ident"])
        MSET("pool", ones_bf, 1.0, ["ones_bf"])
        MSET("pool", ones_f, 1.0, ["ones_f"])
        MSET("pool", eps_t, EPS, ["eps"])
        DMA("sp", cT, cT_d, "c_cT", (), ["cT"])
        DMA("sp", ng, ng_d, "c_ng", (), ["ng"])

        wa = [A.get([128, 2 * D], F32) for _ in range(3)]
        wag = [A.get([128, D], F32) for _ in range(4)]
        bada = A.get([1, 3 * D], F32)
        mod = A.get([1, 3 * D], F32)
        wu = [A.get_top([128, 3, 8, 128], BF16) for _ in range(2)]
        wzb = [A.get_top([128, 8, 128], BF16) for _ in range(2)]
        Etab = [A.get_top([128, 3, 512], BF16) for _ in range(2)]
        bst = A.get_top([128, 512], F32)
        xt = [A.get([128, D], F32) for _ in range(8)]
        xh = [A.get([128, D], BF16) for _ in range(8)]
        junk = A.get([128, D], BF16)

        DMA("sp", bada, bada_d, "c_bada", (), ["bada"])
        def load_wa(kc):
            DMA("sp", wa[kc % 3], wada_d[kc * 128:(kc + 1) * 128, 0:2 * D], f"wa{kc % 3}", (), [("wa", kc % 3)])

        def load_xt(t):
            DMA("sp", xt[t % 8], x[t * 128:(t + 1) * 128, :], f"xt{t % 8}", (), [("xt", t % 8)])

        for kc in range(3):
            load_wa(kc)
        for t in range(8):
            load_xt(t)
        for kc in range(8):
            if kc >= 3:
                load_wa(kc)
            for n in range(4):
                MM(bank(n)[0:1, :], cT[:, kc:kc + 1], wa[kc % 3][:, n * 512:(n + 1) * 512], kc == 0, kc == 7,
                   [("wa", kc % 3), "cT"], [("ps", n)])
        def load_wag(kc):
            DMA("pool", wag[kc % 4], wada_d[kc * 128:(kc + 1) * 128, 2 * D:3 * D], f"wag{kc % 4}", (), [("wag", kc % 4)])

        for kc in range(4):
            load_wag(kc)
        for n in range(4):
            TT("dve", mod[0:1, n * 512:(n + 1) * 512], bank(n)[0:1, :], bada[0:1, n * 512:(n + 1) * 512], ALU.add,
               [("ps", n), "bada"], [("mod", n)])
        for j in range(16):
            MM(bank(6)[:, j:j + 1], mod[0:1, j * 128:(j + 1) * 128], ones_f[0:1, 0:1], True, True,
               [("mod", j // 4), "ones_f"], [("ps", 6)])
        CP("dve", shc, bank(6)[:, 0:8], [("ps", 6)], ["shc"])
        STT(gs, bank(6)[:, 8:16], 1.0, ng, ALU.add, ALU.mult, [("ps", 6), "ng"], ["gs"])

        def gate_mm(kc):
            for n in range(2):
                MM(bank(n)[0:1, :], cT[:, kc:kc + 1], wag[kc % 4][:, n * 512:(n + 1) * 512], kc == 0, kc == 7,
                   [("wag", kc % 4), "cT"], [("ps", n)])

        def gate_part():
            for n in range(2):
                TT("dve", mod[0:1, 2048 + n * 512:2048 + (n + 1) * 512], bank(n)[0:1, :],
                   bada[0:1, 2048 + n * 512:2048 + (n + 1) * 512], ALU.add, [("ps", n), "bada"], [("mod", 4 + n)])
            for n in range(2):
                MM(bank(2 + n), ones_f[0:1, 0:128], mod[0:1, 2048 + n * 512:2048 + (n + 1) * 512], True, True,
                   [("mod", 4 + n), "ones_f"], [("ps", 2 + n)])
                CP("dve", GB[:, n * 512:(n + 1) * 512], bank(2 + n), [("ps", 2 + n)], [("GB", n)])

        def load_unit_weights(u):
            hp_, gi_ = divmod(u, 3)
            par_ = u % 2
            for j in range(3):
                c0 = gi_ * 1536 + j * 512 + hp_ * 128
                DMA("pool", wu[par_][:, j, :, :], win_v[:, :, c0:c0 + 128], f"wu{par_}{j}", (), [("wu", par_, j)])

        def load_hp_tables(hp_):
            for gi_ in range(3):
                DMA("pool", bst, bg_d[:, hp_ * 3 + gi_, :], "bst", (), ["bst"])
                ACT(Etab[hp_ % 2][:, gi_, :], bst, AF.Exp, ["bst"], [("E", hp_ % 2, gi_)])
            c0z = 4608 + hp_ * 128
            DMA("pool", wzb[hp_ % 2], win_v[:, :, c0z:c0z + 128], f"wz{hp_ % 2}", (), [("wz", hp_ % 2)])

        load_unit_weights(0)
        load_hp_tables(0)

        trbanks = (7, 5)
        tri = [0]

        def stageA(grp):
            for i in range(4):
                t = grp * 4 + i
                sl = t % 8
                if t >= 8:
                    load_xt(t)
                STT(junk, xt[sl], 1.0, xt[sl], ALU.mult, ALU.mult, [("xt", sl)], ["junk", ("ss", t)],
                    accum_out=ss[:, t:t + 1])
            g4 = slice(grp * 4, grp * 4 + 4)
            ACT(sd[:, g4], ss[:, g4], AF.Sqrt, [("ss", grp * 4 + i) for i in range(4)] + ["eps"], [("sd", grp)],
                scale=1.0 / D, bias=eps_t[:, 0:1])
            RCP(rstd[:, g4], sd[:, g4], [("sd", grp)], [("rstd", grp)])
            for i in range(4):
                t = grp * 4 + i
                sl = t % 8
                xi = (grp % 2) * 4 + i
                if i < 2:
                    ACT(xh[xi], xt[sl], AF.Identity, [("xt", sl), ("rstd", grp)], [("xh", xi)], scale=rstd[:, t:t + 1])
                else:
                    TS("dve", xh[xi], xt[sl], rstd[:, t:t + 1], None, ALU.mult, None,
                       [("xt", sl), ("rstd", grp)], [("xh", xi)])

        def stageB(grp):
            for k2 in range(4):
                b = trbanks[tri[0] % 2]
                tri[0] += 1
                trb = bank_bf(b)
                for kcl in range(2):
                    kc = k2 * 2 + kcl
                    for i in range(4):
                        xi = (grp % 2) * 4 + i
                        TR(trb[:, kcl * 512 + i * 128:kcl * 512 + (i + 1) * 128], xh[xi][:, kc * 128:(kc + 1) * 128], ident,
                           [("xh", xi), "ident"], [("ps", b)])
                for kcl in range(2):
                    kc = k2 * 2 + kcl
                    if k2 < 3:
                        ACT(hT[:, kc, grp * 512:(grp + 1) * 512], trb[:, kcl * 512:(kcl + 1) * 512], AF.Identity,
                            [("ps", b), "gs", "shc"], [("hT", kc, grp)], scale=gs[:, kc:kc + 1], bias=shc[:, kc:kc + 1])
                    else:
                        TS("dve", hT[:, kc, grp * 512:(grp + 1) * 512], trb[:, kcl * 512:(kcl + 1) * 512],
                           gs[:, kc:kc + 1], shc[:, kc:kc + 1], ALU.mult, ALU.add,
                           [("ps", b), "gs", "shc"], [("hT", kc, grp)])
            gate_mm(grp)
            if grp + 4 < 8:
                load_wag(grp + 4)
            if grp == 7:
                gate_part()

        stageA(0)
        for grp in range(8):
            if grp + 1 < 8:
                stageA(grp + 1)
            stageB(grp)

        def dump(src2d, name):
            P.barrier()
            DMA("sp", dbg_d, src2d, "dbg", [name], ["dbgout"])
            P.op("sp", None, ["dbgout"], ())

        def finish():
            P.finalize()
            sems = {"dma": {}, "eng": {}}
            for slot in P.slotcnt:
                sems["dma"][slot] = es.enter_context(nc.semaphore("d_" + str(slot)))
            for e in P.ENG:
                nep = max(1, (P.nsig[e] + EPOCH - 1) // EPOCH)
                sems["eng"][e] = [es.enter_context(nc.semaphore(f"e_{e}{i}")) for i in range(nep)]
            block = es.enter_context(nc.Block())

            @block.tensor
            def _(eng):
                P.emit("pe", eng, sems)

            @block.scalar
            def _(eng):
                P.emit("act", eng, sems)

            @block.vector
            def _(eng):
                P.emit("dve", eng, sems)

            @block.gpsimd
            def _(eng):
                P.emit("pool", eng, sems)

            @block.sync
            def _(eng):
                P.emit("sp", eng, sems)

        if dbg == "hT":
            P.lastw["hTall"] = None
            dump(hT.rearrange("p a b -> p (a b)"), "hTall")
            finish()
            return nc

        P.barrier()
        A.top = persist_top
        AT = A.get([128, 4, S], BF16)
        p2_top = A.top
        QT = A.get([128, S], BF16)
        KT = A.get([128, S], BF16)
        Vaug = A.get([128, 32, 2, 128], BF16)
        VT = A.get([128, 2048], BF16)
        acc = A.get([128, 2, S], F32)
        Xe = [A.get([128, 2, 256], BF16) for _ in range(2)]
        PT = [A.get([128, 2, 256], BF16) for _ in range(4)]
        zs = A.get([128, 512], F32)
        rec = [A.get([128, 512], F32) for _ in range(2)]

        MSET("pool", Vaug.rearrange("p a b c -> p (a b c)"), 1.0, ["Vones"] + [(("V", k_), h_) for k_ in range(32) for h_ in range(2)])

        PROJ_BANKS = (0, 1, 6, 7)
        STA = (0, 1)
        STB = (2, 3)
        NDB = (4, 5, 6, 7)
        TRB = (2, 3, 4, 5)
        pj = [0]

        def next_proj_bank():
            b = PROJ_BANKS[pj[0] % len(PROJ_BANKS)]
            pj[0] += 1
            return b

        evi = [0]

        def evac_engine():
            evi[0] += 1
            return "act" if evi[0] % 2 == 0 else "dve"

        def finish_slice(pf, s):
            fhp, akeys, fkeys = pf
            wz = wzb[fhp % 2]
            b = next_proj_bank()
            for kc in range(8):
                MM(bank(b), wz[:, kc, :], hT[:, kc, s * 512:(s + 1) * 512], kc == 0, kc == 7,
                   [("wz", fhp % 2), ("hT", kc, s)], [("ps", b)])
            ACT(zs, bank(b), AF.Silu, [("ps", b)], ["zs"])
            sl = slice(s * 512, (s + 1) * 512)
            rc = rec[s % 2]
            ra = ("rec", s % 2, 0)
            rb = ("rec", s % 2, 1)
            DMA("sp", rc[0:64, :], acc[64:128, 0, sl], f"rcA{s % 2}", akeys, [ra, ("accF", fhp, s, 0)])
            DMA("sp", rc[64:128, :], acc[0:64, 1, sl], f"rcB{s % 2}", akeys, [rb, ("accF", fhp, s, 1)])
            RCP(rc, rc, [ra, rb], [ra, rb])
            TT("dve", rc[0:64, :], rc[0:64, :], acc[0:64, 0, sl], ALU.mult, [ra] + akeys, [ra, ("accF", fhp, s, 2)])
            TT("dve", rc[64:128, :], rc[64:128, :], acc[64:128, 1, sl], ALU.mult, [rb] + akeys, [rb, ("accF", fhp, s, 3)])
            TT("pool", AT[:, fhp, sl], rc, zs, ALU.mult, [ra, rb, "zs"], [("AT", fhp)])
            fkeys += [("accF", fhp, s, k) for k in range(4)]

        hps = list(range(4)) if dbg != "AT0" else [0]
        acc_prev = []
        pend_fin = None
        for hp in hps:
            epar = hp % 2
            if hp > 0:
                load_hp_tables(hp)
            for gi in range(3):
                u = hp * 3 + gi
                par = u % 2
                dil = DILS[gi]
                nb = 32 // dil
                Ls = 512 // dil
                if u + 1 < 12 and not (dbg == "AT0" and u + 1 >= 3):
                    load_unit_weights(u + 1)
                for s in range(NS):
                    if pend_fin is not None and gi == 0:
                        finish_slice(pend_fin, s)
                    for j in range(3):
                        b = next_proj_bank()
                        for kc in range(8):
                            MM(bank(b), wu[par][:, j, kc, :], hT[:, kc, s * 512:(s + 1) * 512], kc == 0, kc == 7,
                               [("wu", par, j), ("hT", kc, s)], [("ps", b)])
                        if dil == 1:
                            src = bank(b)
                        else:
                            src = bank(b).rearrange("p (l r) -> p r l", r=dil)
                        if j < 2:
                            T = QT if j == 0 else KT
                            if dil == 1:
                                dst = T[:, s * 512:(s + 1) * 512]
                            else:
                                dst = T.rearrange("p (r l) -> p r l", r=dil)[:, :, s * Ls:(s + 1) * Ls]
                            CP(evac_engine(), dst, src, [("ps", b)], [("QT" if j == 0 else "KT", s)])
                        else:
                            if dil == 1:
                                dst = VT[:, (s % 4) * 512:(s % 4 + 1) * 512]
                            else:
                                dst = VT.rearrange("p (r l) -> p r l", r=dil)[:, :, (s % 4) * Ls:(s % 4 + 1) * Ls]
                            CP(evac_engine(), dst, src, [("ps", b)], [("VT", s % 4)])
                            Lv = 2048 // dil
                            if dil == 1:
                                blks = [[(0, 4 * s + i) for i in range(4)]]
                            elif dil == 4:
                                blks = [[(r, s) for r in range(4)]]
                            else:
                                blks = [[(r0 + i, s // 4) for i in range(4)] for r0 in range(0, 16, 4)] if s % 4 == 3 else []
                            for bl in blks:
                                tb = TRB[(s + bl[0][0] // 4) % 4]
                                trb = bank_bf(tb)
                                for i, (r, n) in enumerate(bl):
                                    off = r * Lv + (n * 128) % Lv
                                    TR(trb[:, i * 128:(i + 1) * 128], VT[:, off:off + 128], ident,
                                       [("VT", q) for q in range(4)] + ["ident"], [("ps", tb)])
                                kb0 = bl[0][0] * nb + bl[0][1]
                                kstep = (bl[1][0] * nb + bl[1][1]) - kb0
                                kbs = slice(kb0, kb0 + 3 * kstep + 1, kstep)
                                srcv = trb[:, 0:512].rearrange("p (a b) -> p a b", a=4)
                                vkeys = [("V", bl[i][0] * nb + bl[i][1]) for i in range(4)]
                                ve = evac_engine()
                                CP(ve, Vaug[:, kbs, 0, 0:64], srcv[:, :, 0:64], [("ps", tb)], [(k, 0) for k in vkeys])
                                CP(ve, Vaug[:, kbs, 1, 64:128], srcv[:, :, 64:128], [("ps", tb)], [(k, 1) for k in vkeys])
                if pend_fin is not None and gi == 0:
                    acc_prev = pend_fin[2]
                    pend_fin = None
                blocks = [(r, n) for r in range(dil) for n in range(nb)]
                NBk = len(blocks)
                qk_reads = [("QT", s_) for s_ in range(NS)] + [("KT", s_) for s_ in range(NS)]
                ev = Etab[epar][:, gi, :].rearrange("p (h c) -> p h c", h=2)
                accv = acc.rearrange("p c (l r) -> p c r l", r=dil)
                cur_keys = []

                def Nof(i):
                    return 256 if blocks[i][1] < nb - 1 else 128

                def S_(i):
                    N = Nof(i)
                    for hb, banks in ((0, STA), (1, STB)):
                        b = banks[i % 2]
                        rows = slice(hb * 64, (hb + 1) * 64)
                        MM(bank(b)[:, 0:N], KT[rows, i * 128:(i + 1) * 128], QT[rows, i * 128:i * 128 + N],
                           True, True, qk_reads, [("ps", b)])

                def X_(i):
                    N = Nof(i)
                    for hb, banks in ((0, STA), (1, STB)):
                        b = banks[i % 2]
                        ACT(Xe[i % 2][:, hb, 0:N], bank(b)[:, 0:N], AF.Exp, [("ps", b)], [("Xe", i % 2, hb)], scale=0.125)

                def M_(i):
                    N = Nof(i)
                    TT("dve", PT[i % 4][:, :, 0:N], Xe[i % 2][:, :, 0:N], ev[:, :, 0:N], ALU.mult,
                       [("Xe", i % 2, 0), ("Xe", i % 2, 1), ("E", epar, gi)], [("PT", i % 4)])

                def V_(i):
                    r, n = blocks[i]
                    nd = NDB[i % 4]
                    for hb in range(2):
                        cols = slice(hb * 128, (hb + 1) * 128)
                        if n > 0:
                            MM(bank(nd)[:, cols], Vaug[:, i - 1, hb, :], PT[(i - 1) % 4][:, hb, 128:256], True, False,
                               [(("V", i - 1), hb), "Vones", ("PT", (i - 1) % 4)], [("ps", nd)])
                        MM(bank(nd)[:, cols], Vaug[:, i, hb, :], PT[i % 4][:, hb, 0:128], n == 0, True,
                           [(("V", i), hb), "Vones", ("PT", i % 4)], [("ps", nd)])

                def A_(i):
                    r, n = blocks[i]
                    nd = NDB[i % 4]
                    dstv = accv[:, :, r, n * 128:(n + 1) * 128]
                    srcv = bank(nd)[:, 0:256].rearrange("p (c q) -> p c q", c=2)
                    key = ("accA", u, i)
                    cur_keys.append(key)
                    if gi == 0:
                        CP("act", dstv, srcv, [("ps", nd)] + acc_prev, [key])
                    else:
                        TT("dve", dstv, srcv, dstv, ALU.add, [("ps", nd)] + acc_prev, [key])

                NBe = NBk
                for step in range(NBe + 4):
                    if step < NBe:
                        S_(step)
                        X_(step)
                        M_(step)
                    if 0 <= step - 2 < NBe:
                        V_(step - 2)
                    if 0 <= step - 4 < NBe:
                        A_(step - 4)
                acc_prev = cur_keys

            if dbg == "acc":
                dump(acc.rearrange("p a b -> p (a b)"), "acc")
                finish()
                return nc
            pend_fin = (hp, list(acc_prev), [])
            if hp == hps[-1]:
                for s_ in range(NS):
                    finish_slice(pend_fin, s_)
                acc_prev = pend_fin[2]
                pend_fin = None

        if dbg in ("AT", "AT0"):
            P.lastw["ATall"] = None
            dump(AT.rearrange("p a b -> p (a b)"), "ATall")
            finish()
            return nc

        P.barrier()
        A.top = p2_top
        A.hi = A.n
        M2T = A.get([128, 4, S], BF16)
        p3_top = A.top
        Up = A.get([128, 16 + S], F32)
        Ta = A.get([128, 16 + S], F32)
        Tb = A.get([128, 16 + S], F32)
        pooled = A.get([128, S], BF16)
        wu2 = [A.get([128, 2, 8, 128], BF16) for _ in range(2)]
        pw = A.get([128, 4, 128], BF16)
        pscale = A.get([128, 4], F32)
        icnt = A.get([128, 64], F32)
        zs3 = [A.get([128, 512], BF16) for _ in range(4)]
        t16 = A.get([128, 16], F32)

        DMA("sp", pscale, pscale_d, "c_pscale", (), ["pscale"])
        DMA("sp", icnt, icnt_d, "c_icnt", (), ["icnt"])
        DMA("pool", pw, poolw_d.rearrange("g c e -> c g e"), "c_pw", (), ["pw"])
        MSET("pool", Up[:, 0:16], 0.0, ["pad"])
        MSET("pool", Ta[:, 0:16], 0.0, ["pad"])
        MSET("pool", Tb[:, 0:16], 0.0, ["pad"])

        def load_pool_weights(pg):
            for j, base in enumerate((5120, 5632)):
                c0 = base + pg * 128
                DMA("pool", wu2[pg % 2][:, j, :, :], win_v[:, :, c0:c0 + 128], f"wu2{pg % 2}{j}", (), [("wu2", pg % 2, j)])

        load_pool_weights(0)
        LA = 3

        def uproj(pg, s):
            w2 = wu2[pg % 2]
            b = next_proj_bank()
            for kc in range(8):
                MM(bank(b), w2[:, 0, kc, :], hT[:, kc, s * 512:(s + 1) * 512], kc == 0, kc == 7,
                   [("wu2", pg % 2, 0), ("hT", kc, s)], [("ps", b)])
            CP(evac_engine(), Up[:, 16 + s * 512:16 + (s + 1) * 512], bank(b), [("ps", b)], [("Up", s)])

        for s in range(NS):
            uproj(0, s)
        upk = [("Up", s_) for s_ in range(NS)]
        for pg in range(4):
            if pg + 1 < 4:
                load_pool_weights(pg + 1)
            w2 = wu2[pg % 2]
            win = 2 ** (pg + 1)
            a, akeys = Up, upk
            for k in range(pg + 1):
                bbuf, bname = (Ta, "Ta") if k % 2 == 0 else (Tb, "Tb")
                sh = 2 ** k
                TT("dve", bbuf[:, 16:16 + S], a[:, 16:16 + S], a[:, 16 - sh:16 - sh + S], ALU.add, akeys + ["pad"], [bname])
                a, akeys = bbuf, [bname]
            STT(pooled, a[:, 16:16 + S], 1.0 / win, Up[:, 16:16 + S], ALU.mult, ALU.subtract, akeys + upk, ["pooled"])
            TT("dve", t16, a[:, 16:32], icnt[:, pg * 16:(pg + 1) * 16], ALU.mult, akeys + ["icnt"], ["t16"])
            TT("dve", pooled[:, 0:16], t16, Up[:, 16:32], ALU.subtract, ["t16", "pooled"] + upk, ["pooled"])
            for step in range(NS + LA):
                if step < NS:
                    s = step
                    sl = slice(s * 512, (s + 1) * 512)
                    b2 = 2 + s % 4
                    for kc in range(8):
                        MM(bank(b2), w2[:, 1, kc, :], hT[:, kc, sl], kc == 0, kc == 7,
                           [("wu2", pg % 2, 1), ("hT", kc, s)], [("ps", b2)])
                    ACT(zs3[s % 4], bank(b2), AF.Silu, [("ps", b2)], [("zs3", s % 4)])
                if step >= LA:
                    s = step - LA
                    sl = slice(s * 512, (s + 1) * 512)
                    b1 = next_proj_bank()
                    MM(bank(b1), pw[:, pg, :], pooled[:, sl], True, True, ["pw", "pooled"], [("ps", b1)])
                    STT(M2T[:, pg, sl], bank(b1), pscale[:, pg:pg + 1], zs3[s % 4], ALU.mult, ALU.mult,
                        [("ps", b1), ("zs3", s % 4), "pscale"], [("M2T", pg)])
                    if pg + 1 < 4:
                        uproj(pg + 1, s)

        if dbg == "M2T":
            P.lastw["M2Tall"] = None
            dump(M2T.rearrange("p a b -> p (a b)"), "M2Tall")
            finish()
            return nc

        P.barrier()
        A.top = p3_top
        wg = A.get([128, 2, 8, D], BF16)
        wab = A.get([128, 4, D], BF16)
        wpb = A.get([128, 4, D], BF16)
        p4_top = A.top
        sg = A.get([128, 2, 8, 512], BF16)
        t1 = [A.get([128, 512], F32) for _ in range(2)]
        t2 = [A.get([128, 512], F32) for _ in range(2)]

        for fb in range(4):
            for wsel in range(2):
                c0g = 6144 + wsel * 1024 + fb * 256
                DMA("pool", wg[:, wsel, :, fb * 256:(fb + 1) * 256], win_v[:, :, c0g:c0g + 256],
                    f"wg{wsel}{fb}", (), [("wg", wsel, fb)])
        DMA("pool", wab, wab_d.rearrange("(c p) f -> p c f", p=128), "c_wab", (), ["wab"])
        DMA("pool", wpb, wpb_d.rearrange("(c p) f -> p c f", p=128), "c_wpb", (), ["wpb"])

        G_BANKS = (0, 1, 2, 3)
        Y_BANKS = (4, 5, 6, 7)
        gi_ = 0
        yi_ = 0
        for s in range(NS):
            sl = slice(s * 512, (s + 1) * 512)
            for f in range(8):
                for wsel in range(2):
                    b = G_BANKS[gi_ % 4]
                    gi_ += 1
                    for kc in range(8):
                        MM(bank(b), wg[:, wsel, kc, f * 128:(f + 1) * 128], hT[:, kc, sl], kc == 0, kc == 7,
                           [("wg", wsel, f // 2), ("hT", kc, s)], [("ps", b)])
                    ACT(sg[:, wsel, f, :], bank(b), AF.Sigmoid, [("ps", b)], [("sg", wsel, f)])
            for f in range(8):
                ba = Y_BANKS[yi_ % 4]
                bp = Y_BANKS[(yi_ + 1) % 4]
                yi_ += 2
                for c in range(4):
                    MM(bank(ba), wab[:, c, f * 128:(f + 1) * 128], AT[:, c, sl], c == 0, c == 3,
                       ["wab", ("AT", c)], [("ps", ba)])
                for c in range(4):
                    MM(bank(bp), wpb[:, c, f * 128:(f + 1) * 128], M2T[:, c, sl], c == 0, c == 3,
                       ["wpb", ("M2T", c)], [("ps", bp)])
                TT("dve", t1[f % 2], bank(ba), sg[:, 0, f, :], ALU.mult, [("ps", ba), ("sg", 0, f)], [("t1", f % 2)])
                TT("dve", t2[f % 2], bank(bp), sg[:, 1, f, :], ALU.mult, [("ps", bp), ("sg", 1, f)], [("t2", f % 2)])
                TT("pool", hT[:, f, sl], t1[f % 2], t2[f % 2], ALU.add, [("t1", f % 2), ("t2", f % 2)], [("hT", f, s)])

        if dbg == "mg":
            P.lastw["hTall"] = None
            dump(hT.rearrange("p a b -> p (a b)"), "hTall")
            finish()
            return nc

        P.barrier()
        A.top = p2_top
        wout = A.get([128, 8, D], BF16)
        FG = A.get([128, D], F32)
        NX2 = 8
        NYT = 6
        x2 = [A.get([128, D], F32) for _ in range(NX2)]
        xn = [A.get([128, D], F32) for _ in range(2)]
        yt = [A.get([128, D], F32) for _ in range(NYT)]
        junk2 = A.get([128, D], BF16)
        ss2 = A.get([128, NT], F32)
        sd2 = A.get([128, NT], F32)
        r2 = A.get([128, NT], F32)

        for m in range(8):
            DMA("pool", wout[:, m, :], wout_d[m * 128:(m + 1) * 128, :], f"c_wout{m}", (), [("wout", m)])
        for m in range(8):
            TT("dve", wout[:, m, :], wout[:, m, :], GB, ALU.mult, [("wout", m), ("GB", 0), ("GB", 1)], [("wout", m)])
        DMA("sp", FG, fg_d, "c_fg", (), ["FG"])
        outs = []
        OB = ((0, 1), (2, 3), (4, 5), (6, 7))
        def load_x2(t_):
            DMA("sp", x2[t_ % NX2], x[t_ * 128:(t_ + 1) * 128, :], f"x2{t_ % NX2}", (), [("x2", t_ % NX2)])

        for t_ in range(NX2 - 1):
            load_x2(t_)
        for tt in range(NT):
            s = tt // 4
            xs = tt % NX2
            if tt + NX2 - 1 < NT:
                load_x2(tt + NX2 - 1)
            xb = xn[tt % 2]
            for half in range(2):
                b = OB[tt % 4][half]
                for m in range(8):
                    MM(bank(b), hT[:, m, tt * 128:(tt + 1) * 128], wout[:, m, half * 512:(half + 1) * 512], m == 0, m == 7,
                       [("hT", m, s), ("wout", m)], [("ps", b)])
                TT("dve", xb[:, half * 512:(half + 1) * 512], bank(b), x2[xs][:, half * 512:(half + 1) * 512], ALU.add,
                   [("ps", b), ("x2", xs)], [("xn", tt % 2, half)])
            ACT(junk2, xb, AF.Square, [("xn", tt % 2, 0), ("xn", tt % 2, 1)], ["junk2", ("ss2", tt)], accum_out=ss2[:, tt:tt + 1])
            ACT(sd2[:, tt:tt + 1], ss2[:, tt:tt + 1], AF.Sqrt, [("ss2", tt), "eps"], [("sd2", tt)], scale=1.0 / D, bias=eps_t[:, 0:1])
            RCP(r2[:, tt:tt + 1], sd2[:, tt:tt + 1], [("sd2", tt)], [("r2", tt)])
            ys = tt % NYT
            STT(yt[ys], xb, r2[:, tt:tt + 1], FG, ALU.mult, ALU.mult, [("xn", tt % 2, 0), ("xn", tt % 2, 1), ("r2", tt), "FG"], [("yt", ys)])
            DMA("sp", y[tt * 128:(tt + 1) * 128, :], yt[ys], f"yt{ys}", [("yt", ys)], [("yout", tt)])
        P.op("sp", None, [("yout", tt) for tt in range(NT)], ())
        finish()
    return nc


def _t5_bucket(n):
    max_exact = 16
    nf = np.maximum(n, 1).astype(np.float32)
    large = max_exact + (np.log(nf / np.float32(max_exact)) / np.float32(math.log(2048 / max_exact))
                         * np.float32(32 - max_exact)).astype(np.int32)
    large = np.minimum(large, 31)
    return np.where(n < max_exact, n, large)


def _const_tables(rel_bias):
    k = np.arange(128)[:, None]
    q = np.arange(128)[None, :]
    dist_prev = np.clip(128 + q - k, 0, 128)
    dist_cur = np.clip(q - k, 0, 128)
    ok_prev = np.broadcast_to(k >= q, (128, 128))
    ok_cur = np.broadcast_to(k <= q, (128, 128))
    bg = np.full((128, 12, 2, 2, 128), -1.0e4, np.float32)
    for gi, dil in enumerate(DILS):
        bp = _t5_bucket(dist_prev * dil)
        bc = _t5_bucket(dist_cur * dil)
        for hp in range(4):
            for hb in range(2):
                col = gi * 8 + hp * 2 + hb
                gc = rel_bias[bc, col]
                gp = rel_bias[bp, col]
                bg[:, hp * 3 + gi, hb, 0, :][ok_cur] = gc[ok_cur]
                bg[:, hp * 3 + gi, hb, 1, :][ok_prev] = gp[ok_prev]
    icnt = np.zeros((128, 4, 16), np.float32)
    for pg in range(4):
        win = 2 ** (pg + 1)
        icnt[:, pg, :] = 1.0 / np.minimum(np.arange(16) + 1, win).astype(np.float32)
    return bg.reshape(128, 12, 512), icnt.reshape(128, 64)


def make_in_maps(inputs):
    f = lambda a: np.ascontiguousarray(np.asarray(a, dtype=np.float32))
    x = f(inputs["x"])
    c = f(inputs["c"])
    bg, icnt = _const_tables(f(inputs["rel_bias"]))
    shared = {
        "ng": f(f(inputs["norm_g"])[0].reshape(8, 128).T),
        "w_ada": f(inputs["w_ada"])[0],
        "b_ada": f(inputs["b_ada"])[0].reshape(1, 3 * D),
        "w_in": f(inputs["w_in"])[0],
        "pool_w": f(inputs["pool_w"])[0],
        "pscale": f(f(inputs["pool_scale"])[0].reshape(4, 128).T),
        "w_ab": f(inputs["w_attn_br"])[0],
        "w_pb": f(inputs["w_pool_br"])[0],
        "w_out": f(inputs["w_out"])[0],
        "bg": bg, "icnt": icnt,
        "ident": np.eye(128, dtype=np.float32),
        "fg": f(np.broadcast_to(f(inputs["final_g"])[None, :], (128, D))),
    }
    maps = []
    for b in range(x.shape[0]):
        m = dict(shared)
        m["x"] = x[b]
        m["cT"] = f(c[b].reshape(8, 128).T)
        maps.append(m)
    return maps


_NC_CACHE = {}


def kernel(**inputs):
    maps = make_in_maps(inputs)
    if "nc" not in _NC_CACHE:
        _NC_CACHE["nc"] = build()
    res = run_bass_kernel_spmd(_NC_CACHE["nc"], maps, core_ids=list(range(8)))
    return np.stack([np.asarray(r["y"], dtype=np.float32) for r in res.results], axis=0)
```

```python
import math
from contextlib import ExitStack

import numpy as np
import concourse.bass as bass
import concourse.mybir as mybir
from concourse.bass_utils import run_bass_kernel_spmd

F32 = mybir.dt.float32
BF16 = mybir.dt.bfloat16
AF = mybir.ActivationFunctionType
ALU = mybir.AluOpType

S = 4096
D = 1024
NT = 32
NS = 8
EPS = 1e-6
DILS = (1, 4, 16)
EPOCH = 4000
SBUF_WORDS = 53200


class Op:
    __slots__ = ("eng", "fn", "slot", "slotn", "signal", "count", "deps", "waits")


class Prog:
    ENG = ("pe", "act", "dve", "pool", "sp")

    def __init__(self):
        self.ops = {e: [] for e in self.ENG}
        self.lastw = {}
        self.readers = {}
        self.slotcnt = {}
        self.lastdma = {}
        self.lastcomp = {}
        self.pending = {e: set() for e in self.ENG}

    def op(self, eng, fn, reads=(), writes=(), slot=None):
        o = Op()
        o.eng = eng
        o.fn = fn
        o.slot = slot
        o.signal = False
        o.count = 0
        o.slotn = 0
        deps = set(self.pending[eng])
        self.pending[eng] = set()
        for k in reads:
            w = self.lastw.get(k)
            if w is not None:
                deps.add(w)
        for k in writes:
            w = self.lastw.get(k)
            if w is not None:
                deps.add(w)
            rd = self.readers.get(k)
            if rd:
                for r in rd.values():
                    if isinstance(r, list):
                        deps.update(r)
                    else:
                        deps.add(r)
        for k in reads:
            rd = self.readers.setdefault(k, {})
            if slot is not None:
                rd.setdefault("_dma", []).append(o)
            else:
                rd[eng] = o
        for k in writes:
            self.lastw[k] = o
            self.readers[k] = {}
        deps.discard(o)
        o.deps = deps
        if slot is not None:
            n = self.slotcnt.get(slot, 0) + 1
            self.slotcnt[slot] = n
            o.slotn = n
            self.lastdma[slot] = o
        else:
            self.lastcomp[eng] = o
        self.ops[eng].append(o)
        return o

    def barrier(self):
        allops = set(self.lastcomp.values()) | set(self.lastdma.values())
        for e in self.ENG:
            self.pending[e] |= allops

    def finalize(self):
        for e in self.ENG:
            for o in self.ops[e]:
                for d in o.deps:
                    if d.slot is None and not (d.eng == "pe" and e == "pe"):
                        d.signal = True
        self.nsig = {}
        for e in self.ENG:
            c = 0
            for o in self.ops[e]:
                if o.slot is None and o.signal:
                    c += 1
                    o.count = c
            self.nsig[e] = c
        for e in self.ENG:
            seen = {}
            for o in self.ops[e]:
                w = {}
                for d in o.deps:
                    if d.slot is not None:
                        key = ("dma", d.slot)
                        val = d.slotn
                    else:
                        if d.eng == "pe" and e == "pe":
                            continue
                        key = ("eng", d.eng)
                        val = d.count
                    if seen.get(key, 0) >= val:
                        continue
                    if w.get(key, 0) < val:
                        w[key] = val
                for k, v in w.items():
                    seen[k] = v
                o.waits = w

    def emit(self, e, eng, sems):
        for o in self.ops[e]:
            for (kind, name), v in o.waits.items():
                if kind == "dma":
                    eng.wait_ge(sems["dma"][name], 16 * v)
                else:
                    ep = (v - 1) // EPOCH
                    eng.wait_ge(sems["eng"][name][ep], v - ep * EPOCH)
            if o.fn is None:
                continue
            inst = o.fn(eng)
            if o.slot is not None:
                inst.then_inc(sems["dma"][o.slot], 16)
            elif o.signal:
                ep = (o.count - 1) // EPOCH
                inst.then_inc(sems["eng"][e][ep], 1)


class Alloc:
    def __init__(self, big, nwords):
        self.big = big
        self.n = nwords
        self.top = 0
        self.hi = nwords

    def get_top(self, shape, dtype):
        save = self.top
        nel = 1
        for d in shape[1:]:
            nel *= d
        nw = ((nel * (4 if dtype == F32 else 2) + 3) // 4 + 7) // 8 * 8
        self.hi -= nw
        self.top = self.hi
        ap = self.get(shape, dtype, _chk=False)
        self.top = save
        assert self.top <= self.hi
        return ap

    def get(self, shape, dtype, _chk=True):
        nel = 1
        for d in shape[1:]:
            nel *= d
        nbytes = nel * (4 if dtype == F32 else 2)
        nw = (nbytes + 3) // 4
        off = self.top
        self.top += (nw + 7) // 8 * 8
        assert self.top <= (self.hi if _chk else self.n), f"SBUF overflow {self.top} > {self.hi}"
        ap = self.big[0:shape[0], off:off + nw]
        if dtype != F32:
            ap = ap.bitcast(dtype)
        if len(shape) > 2:
            names = "abcdef"[: len(shape) - 1]
            pat = "p (" + " ".join(names) + ") -> p " + " ".join(names)
            kw = {names[i]: shape[i + 1] for i in range(len(names))}
            ap = ap.rearrange(pat, **kw)
        return ap


def build(dbg=None):
    nc = bass.Bass("TRN2", target_bir_lowering=False)

    def din(name, shape):
        return nc.dram_tensor(name, shape, F32, kind="ExternalInput").ap()

    x = din("x", [S, D])
    cT_d = din("cT", [128, 8])
    ng_d = din("ng", [128, 8])
    wada_d = din("w_ada", [D, 3 * D])
    bada_d = din("b_ada", [1, 3 * D])
    win_d = din("w_in", [D, 8192])
    poolw_d = din("pool_w", [4, 128, 128])
    pscale_d = din("pscale", [128, 4])
    wab_d = din("w_ab", [512, D])
    wpb_d = din("w_pb", [512, D])
    wout_d = din("w_out", [D, D])
    bg_d = din("bg", [128, 12, 512])
    ident_d = din("ident", [128, 128])
    icnt_d = din("icnt", [128, 64])
    fg_d = din("fg", [128, D])
    y = nc.dram_tensor("y", [S, D], F32, kind="ExternalOutput").ap()
    dbg_d = None
    if dbg is not None:
        dshape = {"hT": ([128, 8 * S], BF16), "AT": ([128, 4 * S], BF16), "M2T": ([128, 4 * S], BF16),
                  "mg": ([128, 8 * S], BF16), "AT0": ([128, 4 * S], BF16), "acc": ([128, 2 * S], F32)}[dbg]
        dbg_d = nc.dram_tensor("dbg", dshape[0], dshape[1], kind="ExternalOutput").ap()

    win_v = win_d.rearrange("(kc p) f -> p kc f", p=128)

    P = Prog()
    es = ExitStack()
    with es:
        big = es.enter_context(nc.sbuf_tensor("big", [128, SBUF_WORDS], F32))
        psall = es.enter_context(nc.psum_tensor("psall", [128, 4096], F32))
        A = Alloc(big, SBUF_WORDS)

        def bank(b):
            return psall[:, b * 512:(b + 1) * 512]

        def bank_bf(b):
            return psall[:, b * 512:(b + 1) * 512].bitcast(BF16)

        def MM(out, lhsT, rhs, start, stop, reads, writes):
            P.op("pe", lambda e: e.matmul(out=out, lhsT=lhsT, rhs=rhs, start=start, stop=stop), reads, writes)

        def TR(out, in_, ident, reads, writes):
            P.op("pe", lambda e: e.transpose(out=out, in_=in_, identity=ident), reads, writes)

        def ACT(out, in_, func, reads, writes, scale=None, bias=None, accum_out=None):
            kw = {}
            if scale is not None:
                kw["scale"] = scale
            if bias is not None:
                kw["bias"] = bias
            if accum_out is not None:
                kw["accum_out"] = accum_out
            P.op("act", lambda e: e.activation(out=out, in_=in_, func=func, **kw), reads, writes)

        def TT(eng, out, in0, in1, op, reads, writes):
            P.op(eng, lambda e: e.tensor_tensor(out=out, in0=in0, in1=in1, op=op), reads, writes)

        def TS(eng, out, in0, s1, s2, op0, op1, reads, writes):
            if op1 is None:
                P.op(eng, lambda e: e.tensor_scalar(out=out, in0=in0, scalar1=s1, scalar2=None, op0=op0), reads, writes)
            else:
                P.op(eng, lambda e: e.tensor_scalar(out=out, in0=in0, scalar1=s1, scalar2=s2, op0=op0, op1=op1), reads, writes)

        def STT(out, in0, scalar, in1, op0, op1, reads, writes, accum_out=None):
            if accum_out is None:
                P.op("dve", lambda e: e.scalar_tensor_tensor(out=out, in0=in0, scalar=scalar, in1=in1, op0=op0, op1=op1), reads, writes)
            else:
                P.op("dve", lambda e: e.scalar_tensor_tensor(out=out, in0=in0, scalar=scalar, in1=in1, op0=op0, op1=op1,
                                                             accum_out=accum_out), reads, writes)

        def CP(eng, out, in_, reads, writes):
            if eng == "act":
                ACT(out, in_, AF.Copy, reads, writes)
            else:
                P.op(eng, lambda e: e.tensor_copy(out=out, in_=in_), reads, writes)

        def RCP(out, in_, reads, writes):
            P.op("dve", lambda e: e.reciprocal(out=out, in_=in_), reads, writes)

        def MSET(eng, ap, val, writes):
            P.op(eng, lambda e: e.memset(ap, val), (), writes)

        def DMA(q, out, in_, slot, reads, writes):
            P.op(q, lambda e: e.dma_start(out=out, in_=in_), reads, writes, slot=slot)

        hT = A.get([128, 8, S], BF16)
        GB = A.get([128, D], F32)
        ident = A.get([128, 128], BF16)
        ones_bf = A.get([128, 64], BF16)
        ones_f = A.get([1, 128], F32)
        eps_t = A.get([128, 1], F32)
        cT = A.get([128, 8], F32)
        ng = A.get([128, 8], F32)
        gs = A.get([128, 8], F32)
        shc = A.get([128, 8], F32)
        ss = A.get([128, NT], F32)
        sd = A.get([128, NT], F32)
        rstd = A.get([128, NT], F32)
        persist_top = A.top

        DMA("pool", ident, ident_d, "c_ident", (), ["ident"])
        MSET("pool", ones_bf, 1.0, ["ones_bf"])
        MSET("pool", ones_f, 1.0, ["ones_f"])
        MSET("pool", eps_t, EPS, ["eps"])
        DMA("sp", cT, cT_d, "c_cT", (), ["cT"])
        DMA("sp", ng, ng_d, "c_ng", (), ["ng"])

        wa = [A.get([128, 2 * D], F32) for _ in range(3)]
        wag = [A.get([128, D], F32) for _ in range(4)]
        bada = A.get([1, 3 * D], F32)
        mod = A.get([1, 3 * D], F32)
        wu = [A.get_top([128, 3, 8, 128], BF16) for _ in range(2)]
        wzb = [A.get_top([128, 8, 128], BF16) for _ in range(2)]
        Etab = [A.get_top([128, 3, 512], BF16) for _ in range(2)]
        bst = A.get_top([128, 512], F32)
        xt = [A.get([128, D], F32) for _ in range(8)]
        xh = [A.get([128, D], BF16) for _ in range(8)]
        junk = A.get([128, D], BF16)

        DMA("sp", bada, bada_d, "c_bada", (), ["bada"])
        def load_wa(kc):
            DMA("sp", wa[kc % 3], wada_d[kc * 128:(kc + 1) * 128, 0:2 * D], f"wa{kc % 3}", (), [("wa", kc % 3)])

        def load_xt(t):
            DMA("sp", xt[t % 8], x[t * 128:(t + 1) * 128, :], f"xt{t % 8}", (), [("xt", t % 8)])

        for kc in range(3):
            load_wa(kc)
        for t in range(8):
            load_xt(t)
        for kc in range(8):
            if kc >= 3:
                load_wa(kc)
            for n in range(4):
                MM(bank(n)[0:1, :], cT[:, kc:kc + 1], wa[kc % 3][:, n * 512:(n + 1) * 512], kc == 0, kc == 7,
                   [("wa", kc % 3), "cT"], [("ps", n)])
        def load_wag(kc):
            DMA("pool", wag[kc % 4], wada_d[kc * 128:(kc + 1) * 128, 2 * D:3 * D], f"wag{kc % 4}", (), [("wag", kc % 4)])

        for kc in range(4):
            load_wag(kc)
        for n in range(4):
            TT("dve", mod[0:1, n * 512:(n + 1) * 512], bank(n)[0:1, :], bada[0:1, n * 512:(n + 1) * 512], ALU.add,
               [("ps", n), "bada"], [("mod", n)])
        for j in range(16):
            MM(bank(6)[:, j:j + 1], mod[0:1, j * 128:(j + 1) * 128], ones_f[0:1, 0:1], True, True,
               [("mod", j // 4), "ones_f"], [("ps", 6)])
        CP("dve", shc, bank(6)[:, 0:8], [("ps", 6)], ["shc"])
        STT(gs, bank(6)[:, 8:16], 1.0, ng, ALU.add, ALU.mult, [("ps", 6), "ng"], ["gs"])

        def gate_mm(kc):
            for n in range(2):
                MM(bank(n)[0:1, :], cT[:, kc:kc + 1], wag[kc % 4][:, n * 512:(n + 1) * 512], kc == 0, kc == 7,
                   [("wag", kc % 4), "cT"], [("ps", n)])

        def gate_part():
            for n in range(2):
                TT("dve", mod[0:1, 2048 + n * 512:2048 + (n + 1) * 512], bank(n)[0:1, :],
                   bada[0:1, 2048 + n * 512:2048 + (n + 1) * 512], ALU.add, [("ps", n), "bada"], [("mod", 4 + n)])
            for n in range(2):
                MM(bank(2 + n), ones_f[0:1, 0:128], mod[0:1, 2048 + n * 512:2048 + (n + 1) * 512], True, True,
                   [("mod", 4 + n), "ones_f"], [("ps", 2 + n)])
                CP("dve", GB[:, n * 512:(n + 1) * 512], bank(2 + n), [("ps", 2 + n)], [("GB", n)])

        def load_unit_weights(u):
            hp_, gi_ = divmod(u, 3)
            par_ = u % 2
            for j in range(3):
                c0 = gi_ * 1536 + j * 512 + hp_ * 128
                DMA("pool", wu[par_][:, j, :, :], win_v[:, :, c0:c0 + 128], f"wu{par_}{j}", (), [("wu", par_, j)])

        def load_hp_tables(hp_):
            for gi_ in range(3):
                DMA("pool", bst, bg_d[:, hp_ * 3 + gi_, :], "bst", (), ["bst"])
                ACT(Etab[hp_ % 2][:, gi_, :], bst, AF.Exp, ["bst"], [("E", hp_ % 2, gi_)])
            c0z = 4608 + hp_ * 128
            DMA("pool", wzb[hp_ % 2], win_v[:, :, c0z:c0z + 128], f"wz{hp_ % 2}", (), [("wz", hp_ % 2)])

        load_unit_weights(0)
        load_hp_tables(0)

        trbanks = (7, 5, 4, 6)
        tri = [0]

        def stageA(grp):
            for i in range(4):
                t = grp * 4 + i
                sl = t % 8
                if t >= 8:
                    load_xt(t)
                STT(junk, xt[sl], 1.0, xt[sl], ALU.mult, ALU.mult, [("xt", sl)], ["junk", ("ss", t)],
                    accum_out=ss[:, t:t + 1])
            g4 = slice(grp * 4, grp * 4 + 4)
            ACT(sd[:, g4], ss[:, g4], AF.Sqrt, [("ss", grp * 4 + i) for i in range(4)] + ["eps"], [("sd", grp)],
                scale=1.0 / D, bias=eps_t[:, 0:1])
            RCP(rstd[:, g4], sd[:, g4], [("sd", grp)], [("rstd", grp)])
            for i in range(4):
                t = grp * 4 + i
                sl = t % 8
                xi = (grp % 2) * 4 + i
                if i < 2:
                    ACT(xh[xi], xt[sl], AF.Identity, [("xt", sl), ("rstd", grp)], [("xh", xi)], scale=rstd[:, t:t + 1])
                else:
                    TS("dve", xh[xi], xt[sl], rstd[:, t:t + 1], None, ALU.mult, None,
                       [("xt", sl), ("rstd", grp)], [("xh", xi)])

        def stageB(grp):
            for k2 in range(4):
                b = trbanks[tri[0] % 4]
                tri[0] += 1
                trb = bank_bf(b)
                for kcl in range(2):
                    kc = k2 * 2 + kcl
                    for i in range(4):
                        xi = (grp % 2) * 4 + i
                        TR(trb[:, kcl * 512 + i * 128:kcl * 512 + (i + 1) * 128], xh[xi][:, kc * 128:(kc + 1) * 128], ident,
                           [("xh", xi), "ident"], [("ps", b)])
                for kcl in range(2):
                    kc = k2 * 2 + kcl
                    if k2 < 3:
                        ACT(hT[:, kc, grp * 512:(grp + 1) * 512], trb[:, kcl * 512:(kcl + 1) * 512], AF.Identity,
                            [("ps", b), "gs", "shc"], [("hT", kc, grp)], scale=gs[:, kc:kc + 1], bias=shc[:, kc:kc + 1])
                    else:
                        TS("dve", hT[:, kc, grp * 512:(grp + 1) * 512], trb[:, kcl * 512:(kcl + 1) * 512],
                           gs[:, kc:kc + 1], shc[:, kc:kc + 1], ALU.mult, ALU.add,
                           [("ps", b), "gs", "shc"], [("hT", kc, grp)])
            gate_mm(grp)
            if grp + 4 < 8:
                load_wag(grp + 4)
            if grp == 7:
                gate_part()

        stageA(0)
        for grp in range(8):
            if grp + 1 < 8:
                stageA(grp + 1)
            stageB(grp)

        def dump(src2d, name):
            P.barrier()
            DMA("sp", dbg_d, src2d, "dbg", [name], ["dbgout"])
            P.op("sp", None, ["dbgout"], ())

        def finish():
            P.finalize()
            sems = {"dma": {}, "eng": {}}
            for slot in P.slotcnt:
                sems["dma"][slot] = es.enter_context(nc.semaphore("d_" + str(slot)))
            for e in P.ENG:
                nep = max(1, (P.nsig[e] + EPOCH - 1) // EPOCH)
                sems["eng"][e] = [es.enter_context(nc.semaphore(f"e_{e}{i}")) for i in range(nep)]
            block = es.enter_context(nc.Block())

            @block.tensor
            def _(eng):
                P.emit("pe", eng, sems)

            @block.scalar
            def _(eng):
                P.emit("act", eng, sems)

            @block.vector
            def _(eng):
                P.emit("dve", eng, sems)

            @block.gpsimd
            def _(eng):
                P.emit("pool", eng, sems)

            @block.sync
            def _(eng):
                P.emit("sp", eng, sems)

        if dbg == "hT":
            P.lastw["hTall"] = None
            dump(hT.rearrange("p a b -> p (a b)"), "hTall")
            finish()
            return nc

        P.barrier()
        A.top = persist_top
        AT = A.get([128, 4, S], BF16)
        p2_top = A.top
        QT = A.get([128, S], BF16)
        KT = A.get([128, S], BF16)
        Vaug = A.get([128, 32, 2, 128], BF16)
        VT = A.get([128, 2048], BF16)
        acc = A.get([128, 2, S], F32)
        Xe = [A.get([128, 2, 256], BF16) for _ in range(2)]
        PT = [A.get([128, 2, 256], BF16) for _ in range(4)]
        zs = A.get([128, 512], F32)
        rec = [A.get([128, 512], F32) for _ in range(2)]

        MSET("pool", Vaug.rearrange("p a b c -> p (a b c)"), 1.0, ["Vones"] + [(("V", k_), h_) for k_ in range(32) for h_ in range(2)])

        PROJ_BANKS = (0, 1, 6, 7)
        STA = (0, 1)
        STB = (2, 3)
        NDB = (4, 5, 6, 7)
        TRB = (2, 3, 4, 5)
        pj = [0]

        def next_proj_bank():
            b = PROJ_BANKS[pj[0] % len(PROJ_BANKS)]
            pj[0] += 1
            return b

        evi = [0]

        def evac_engine():
            evi[0] += 1
            return "act" if evi[0] % 2 == 0 else "dve"

        def finish_slice(pf, s):
            fhp, akeys, fkeys = pf
            wz = wzb[fhp % 2]
            b = next_proj_bank()
            for kc in range(8):
                MM(bank(b), wz[:, kc, :], hT[:, kc, s * 512:(s + 1) * 512], kc == 0, kc == 7,
                   [("wz", fhp % 2), ("hT", kc, s)], [("ps", b)])
            ACT(zs, bank(b), AF.Silu, [("ps", b)], ["zs"])
            sl = slice(s * 512, (s + 1) * 512)
            rc = rec[s % 2]
            ra = ("rec", s % 2, 0)
            rb = ("rec", s % 2, 1)
            DMA("sp", rc[0:64, :], acc[64:128, 0, sl], f"rcA{s % 2}", akeys, [ra, ("accF", fhp, s, 0)])
            DMA("sp", rc[64:128, :], acc[0:64, 1, sl], f"rcB{s % 2}", akeys, [rb, ("accF", fhp, s, 1)])
            RCP(rc, rc, [ra, rb], [ra, rb])
            TT("dve", rc[0:64, :], rc[0:64, :], acc[0:64, 0, sl], ALU.mult, [ra] + akeys, [ra, ("accF", fhp, s, 2)])
            TT("dve", rc[64:128, :], rc[64:128, :], acc[64:128, 1, sl], ALU.mult, [rb] + akeys, [rb, ("accF", fhp, s, 3)])
            TT("pool", AT[:, fhp, sl], rc, zs, ALU.mult, [ra, rb, "zs"], [("AT", fhp)])
            fkeys += [("accF", fhp, s, k) for k in range(4)]

        hps = list(range(4)) if dbg != "AT0" else [0]
        acc_prev = []
        pend_fin = None
        for hp in hps:
            epar = hp % 2
            if hp > 0:
                load_hp_tables(hp)
            for gi in range(3):
                u = hp * 3 + gi
                par = u % 2
                dil = DILS[gi]
                nb = 32 // dil
                Ls = 512 // dil
                if u + 1 < 12 and not (dbg == "AT0" and u + 1 >= 3):
                    load_unit_weights(u + 1)
                for s in range(NS):
                    if pend_fin is not None and gi == 0:
                        finish_slice(pend_fin, s)
                    for j in range(3):
                        b = next_proj_bank()
                        for kc in range(8):
                            MM(bank(b), wu[par][:, j, kc, :], hT[:, kc, s * 512:(s + 1) * 512], kc == 0, kc == 7,
                               [("wu", par, j), ("hT", kc, s)], [("ps", b)])
                        if dil == 1:
                            src = bank(b)
                        else:
                            src = bank(b).rearrange("p (l r) -> p r l", r=dil)
                        if j < 2:
                            T = QT if j == 0 else KT
                            if dil == 1:
                                dst = T[:, s * 512:(s + 1) * 512]
                            else:
                                dst = T.rearrange("p (r l) -> p r l", r=dil)[:, :, s * Ls:(s + 1) * Ls]
                            CP(evac_engine(), dst, src, [("ps", b)], [("QT" if j == 0 else "KT", s)])
                        else:
                            if dil == 1:
                                dst = VT[:, (s % 4) * 512:(s % 4 + 1) * 512]
                            else:
                                dst = VT.rearrange("p (r l) -> p r l", r=dil)[:, :, (s % 4) * Ls:(s % 4 + 1) * Ls]
                            CP(evac_engine(), dst, src, [("ps", b)], [("VT", s % 4)])
                            Lv = 2048 // dil
                            if dil == 1:
                                blks = [[(0, 4 * s + i) for i in range(4)]]
                            elif dil == 4:
                                blks = [[(r, s) for r in range(4)]]
                            else:
                                blks = [[(r0 + i, s // 4) for i in range(4)] for r0 in range(0, 16, 4)] if s % 4 == 3 else []
                            for bl in blks:
                                tb = TRB[(s + bl[0][0] // 4) % 4]
                                trb = bank_bf(tb)
                                for i, (r, n) in enumerate(bl):
                                    off = r * Lv + (n * 128) % Lv
                                    TR(trb[:, i * 128:(i + 1) * 128], VT[:, off:off + 128], ident,
                                       [("VT", q) for q in range(4)] + ["ident"], [("ps", tb)])
                                kb0 = bl[0][0] * nb + bl[0][1]
                                kstep = (bl[1][0] * nb + bl[1][1]) - kb0
                                kbs = slice(kb0, kb0 + 3 * kstep + 1, kstep)
                                srcv = trb[:, 0:512].rearrange("p (a b) -> p a b", a=4)
                                vkeys = [("V", bl[i][0] * nb + bl[i][1]) for i in range(4)]
                                ve = evac_engine()
                                CP(ve, Vaug[:, kbs, 0, 0:64], srcv[:, :, 0:64], [("ps", tb)], [(k, 0) for k in vkeys])
                                CP(ve, Vaug[:, kbs, 1, 64:128], srcv[:, :, 64:128], [("ps", tb)], [(k, 1) for k in vkeys])
                if pend_fin is not None and gi == 0:
                    acc_prev = pend_fin[2]
                    pend_fin = None
                blocks = [(r, n) for r in range(dil) for n in range(nb)]
                NBk = len(blocks)
                qk_reads = [("QT", s_) for s_ in range(NS)] + [("KT", s_) for s_ in range(NS)]
                ev = Etab[epar][:, gi, :].rearrange("p (h c) -> p h c", h=2)
                accv = acc.rearrange("p c (l r) -> p c r l", r=dil)
                cur_keys = []

                def Nof(i):
                    return 256 if blocks[i][1] < nb - 1 else 128

                def S_(i):
                    N = Nof(i)
                    for hb, banks in ((0, STA), (1, STB)):
                        b = banks[i % 2]
                        rows = slice(hb * 64, (hb + 1) * 64)
                        MM(bank(b)[:, 0:N], KT[rows, i * 128:(i + 1) * 128], QT[rows, i * 128:i * 128 + N],
                           True, True, qk_reads, [("ps", b)])

                def X_(i):
                    N = Nof(i)
                    for hb, banks in ((0, STA), (1, STB)):
                        b = banks[i % 2]
                        ACT(Xe[i % 2][:, hb, 0:N], bank(b)[:, 0:N], AF.Exp, [("ps", b)], [("Xe", i % 2, hb)], scale=0.125)

                def M_(i):
                    N = Nof(i)
                    TT("dve", PT[i % 4][:, :, 0:N], Xe[i % 2][:, :, 0:N], ev[:, :, 0:N], ALU.mult,
                       [("Xe", i % 2, 0), ("Xe", i % 2, 1), ("E", epar, gi)], [("PT", i % 4)])

                def V_(i):
                    r, n = blocks[i]
                    nd = NDB[i % 4]
                    for hb in range(2):
                        cols = slice(hb * 128, (hb + 1) * 128)
                        if n > 0:
                            MM(bank(nd)[:, cols], Vaug[:, i - 1, hb, :], PT[(i - 1) % 4][:, hb, 128:256], True, False,
                               [(("V", i - 1), hb), "Vones", ("PT", (i - 1) % 4)], [("ps", nd)])
                        MM(bank(nd)[:, cols], Vaug[:, i, hb, :], PT[i % 4][:, hb, 0:128], n == 0, True,
                           [(("V", i), hb), "Vones", ("PT", i % 4)], [("ps", nd)])

                def A_(i):
                    r, n = blocks[i]
                    nd = NDB[i % 4]
                    dstv = accv[:, :, r, n * 128:(n + 1) * 128]
                    srcv = bank(nd)[:, 0:256].rearrange("p (c q) -> p c q", c=2)
                    key = ("accA", u, i)
                    cur_keys.append(key)
                    if gi == 0:
                        CP("act", dstv, srcv, [("ps", nd)] + acc_prev, [key])
                    else:
                        TT("dve", dstv, srcv, dstv, ALU.add, [("ps", nd)] + acc_prev, [key])

                NBe = NBk
                for step in range(NBe + 4):
                    if step < NBe:
                        S_(step)
                        X_(step)
                        M_(step)
                    if 0 <= step - 2 < NBe:
                        V_(step - 2)
                    if 0 <= step - 4 < NBe:
                        A_(step - 4)
                acc_prev = cur_keys

            if dbg == "acc":
                dump(acc.rearrange("p a b -> p (a b)"), "acc")
                finish()
                return nc
            pend_fin = (hp, list(acc_prev), [])
            if hp == hps[-1]:
                for s_ in range(NS):
                    finish_slice(pend_fin, s_)
                acc_prev = pend_fin[2]
                pend_fin = None

        if dbg in ("AT", "AT0"):
            P.lastw["ATall"] = None
            dump(AT.rearrange("p a b -> p (a b)"), "ATall")
            finish()
            return nc

        P.barrier()
        A.top = p2_top
        A.hi = A.n
        M2T = A.get([128, 4, S], BF16)
        p3_top = A.top
        Up = A.get([128, 16 + S], F32)
        Ta = A.get([128, 16 + S], F32)
        Tb = A.get([128, 16 + S], F32)
        pooled = A.get([128, S], BF16)
        wu2 = [A.get([128, 2, 8, 128], BF16) for _ in range(2)]
        pw = A.get([128, 4, 128], BF16)
        pscale = A.get([128, 4], F32)
        icnt = A.get([128, 64], F32)
        zs3 = [A.get([128, 512], BF16) for _ in range(4)]
        t16 = A.get([128, 16], F32)

        DMA("sp", pscale, pscale_d, "c_pscale", (), ["pscale"])
        DMA("sp", icnt, icnt_d, "c_icnt", (), ["icnt"])
        DMA("pool", pw, poolw_d.rearrange("g c e -> c g e"), "c_pw", (), ["pw"])
        MSET("pool", Up[:, 0:16], 0.0, ["pad"])
        MSET("pool", Ta[:, 0:16], 0.0, ["pad"])
        MSET("pool", Tb[:, 0:16], 0.0, ["pad"])

        def load_pool_weights(pg):
            for j, base in enumerate((5120, 5632)):
                c0 = base + pg * 128
                DMA("pool", wu2[pg % 2][:, j, :, :], win_v[:, :, c0:c0 + 128], f"wu2{pg % 2}{j}", (), [("wu2", pg % 2, j)])

        load_pool_weights(0)
        LA = 3

        def uproj(pg, s):
            w2 = wu2[pg % 2]
            b = next_proj_bank()
            for kc in range(8):
                MM(bank(b), w2[:, 0, kc, :], hT[:, kc, s * 512:(s + 1) * 512], kc == 0, kc == 7,
                   [("wu2", pg % 2, 0), ("hT", kc, s)], [("ps", b)])
            CP(evac_engine(), Up[:, 16 + s * 512:16 + (s + 1) * 512], bank(b), [("ps", b)], [("Up", s)])

        for s in range(NS):
            uproj(0, s)
        upk = [("Up", s_) for s_ in range(NS)]
        for pg in range(4):
            if pg + 1 < 4:
                load_pool_weights(pg + 1)
            w2 = wu2[pg % 2]
            win = 2 ** (pg + 1)
            a, akeys = Up, upk
            for k in range(pg + 1):
                bbuf, bname = (Ta, "Ta") if k % 2 == 0 else (Tb, "Tb")
                sh = 2 ** k
                TT("dve", bbuf[:, 16:16 + S], a[:, 16:16 + S], a[:, 16 - sh:16 - sh + S], ALU.add, akeys + ["pad"], [bname])
                a, akeys = bbuf, [bname]
            STT(pooled, a[:, 16:16 + S], 1.0 / win, Up[:, 16:16 + S], ALU.mult, ALU.subtract, akeys + upk, ["pooled"])
            TT("dve", t16, a[:, 16:32], icnt[:, pg * 16:(pg + 1) * 16], ALU.mult, akeys + ["icnt"], ["t16"])
            TT("dve", pooled[:, 0:16], t16, Up[:, 16:32], ALU.subtract, ["t16", "pooled"] + upk, ["pooled"])
            for step in range(NS + LA):
                if step < NS:
                    s = step
                    sl = slice(s * 512, (s + 1) * 512)
                    b2 = 2 + s % 4
                    for kc in range(8):
                        MM(bank(b2), w2[:, 1, kc, :], hT[:, kc, sl], kc == 0, kc == 7,
                           [("wu2", pg % 2, 1), ("hT", kc, s)], [("ps", b2)])
                    ACT(zs3[s % 4], bank(b2), AF.Silu, [("ps", b2)], [("zs3", s % 4)])
                if step >= LA:
                    s = step - LA
                    sl = slice(s * 512, (s + 1) * 512)
                    b1 = next_proj_bank()
                    MM(bank(b1), pw[:, pg, :], pooled[:, sl], True, True, ["pw", "pooled"], [("ps", b1)])
                    STT(M2T[:, pg, sl], bank(b1), pscale[:, pg:pg + 1], zs3[s % 4], ALU.mult, ALU.mult,
                        [("ps", b1), ("zs3", s % 4), "pscale"], [("M2T", pg)])
                    if pg + 1 < 4:
                        uproj(pg + 1, s)

        if dbg == "M2T":
            P.lastw["M2Tall"] = None
            dump(M2T.rearrange("p a b -> p (a b)"), "M2Tall")
            finish()
            return nc

        P.barrier()
        A.top = p3_top
        wg = A.get([128, 2, 8, D], BF16)
        wab = A.get([128, 4, D], BF16)
        wpb = A.get([128, 4, D], BF16)
        p4_top = A.top
        sg = A.get([128, 2, 8, 512], BF16)
        t1 = [A.get([128, 512], F32) for _ in range(2)]
        t2 = [A.get([128, 512], F32) for _ in range(2)]

        for fb in range(4):
            for wsel in range(2):
                c0g = 6144 + wsel * 1024 + fb * 256
                DMA("pool", wg[:, wsel, :, fb * 256:(fb + 1) * 256], win_v[:, :, c0g:c0g + 256],
                    f"wg{wsel}{fb}", (), [("wg", wsel, fb)])
        DMA("pool", wab, wab_d.rearrange("(c p) f -> p c f", p=128), "c_wab", (), ["wab"])
        DMA("pool", wpb, wpb_d.rearrange("(c p) f -> p c f", p=128), "c_wpb", (), ["wpb"])

        G_BANKS = (0, 1, 2, 3)
        Y_BANKS = (4, 5, 6, 7)
        gi_ = 0
        yi_ = 0
        for s in range(NS):
            sl = slice(s * 512, (s + 1) * 512)
            for f in range(8):
                for wsel in range(2):
                    b = G_BANKS[gi_ % 4]
                    gi_ += 1
                    for kc in range(8):
                        MM(bank(b), wg[:, wsel, kc, f * 128:(f + 1) * 128], hT[:, kc, sl], kc == 0, kc == 7,
                           [("wg", wsel, f // 2), ("hT", kc, s)], [("ps", b)])
                    ACT(sg[:, wsel, f, :], bank(b), AF.Sigmoid, [("ps", b)], [("sg", wsel, f)])
            for f in range(8):
                ba = Y_BANKS[yi_ % 4]
                bp = Y_BANKS[(yi_ + 1) % 4]
                yi_ += 2
                for c in range(4):
                    MM(bank(ba), wab[:, c, f * 128:(f + 1) * 128], AT[:, c, sl], c == 0, c == 3,
                       ["wab", ("AT", c)], [("ps", ba)])
                for c in range(4):
                    MM(bank(bp), wpb[:, c, f * 128:(f + 1) * 128], M2T[:, c, sl], c == 0, c == 3,
                       ["wpb", ("M2T", c)], [("ps", bp)])
                TT("dve", t1[f % 2], bank(ba), sg[:, 0, f, :], ALU.mult, [("ps", ba), ("sg", 0, f)], [("t1", f % 2)])
                TT("dve", t2[f % 2], bank(bp), sg[:, 1, f, :], ALU.mult, [("ps", bp), ("sg", 1, f)], [("t2", f % 2)])
                TT("pool", hT[:, f, sl], t1[f % 2], t2[f % 2], ALU.add, [("t1", f % 2), ("t2", f % 2)], [("hT", f, s)])

        if dbg == "mg":
            P.lastw["hTall"] = None
            dump(hT.rearrange("p a b -> p (a b)"), "hTall")
            finish()
            return nc

        P.barrier()
        A.top = p2_top
        wout = A.get([128, 8, D], BF16)
        FG = A.get([128, D], F32)
        NX2 = 8
        NYT = 6
        x2 = [A.get([128, D], F32) for _ in range(NX2)]
        xn = [A.get([128, D], F32) for _ in range(2)]
        yt = [A.get([128, D], F32) for _ in range(NYT)]
        junk2 = A.get([128, D], BF16)
        ss2 = A.get([128, NT], F32)
        sd2 = A.get([128, NT], F32)
        r2 = A.get([128, NT], F32)

        for m in range(8):
            DMA("pool", wout[:, m, :], wout_d[m * 128:(m + 1) * 128, :], f"c_wout{m}", (), [("wout", m)])
        for m in range(8):
            TT("dve", wout[:, m, :], wout[:, m, :], GB, ALU.mult, [("wout", m), ("GB", 0), ("GB", 1)], [("wout", m)])
        DMA("sp", FG, fg_d, "c_fg", (), ["FG"])
        outs = []
        OB = ((0, 1), (2, 3), (4, 5), (6, 7))
        def load_x2(t_):
            DMA("sp", x2[t_ % NX2], x[t_ * 128:(t_ + 1) * 128, :], f"x2{t_ % NX2}", (), [("x2", t_ % NX2)])

        for t_ in range(NX2 - 1):
            load_x2(t_)
        for tt in range(NT):
            s = tt // 4
            xs = tt % NX2
            if tt + NX2 - 1 < NT:
                load_x2(tt + NX2 - 1)
            xb = xn[tt % 2]
            for half in range(2):
                b = OB[tt % 4][half]
                for m in range(8):
                    MM(bank(b), hT[:, m, tt * 128:(tt + 1) * 128], wout[:, m, half * 512:(half + 1) * 512], m == 0, m == 7,
                       [("hT", m, s), ("wout", m)], [("ps", b)])
                TT("dve", xb[:, half * 512:(half + 1) * 512], bank(b), x2[xs][:, half * 512:(half + 1) * 512], ALU.add,
                   [("ps", b), ("x2", xs)], [("xn", tt % 2, half)])
            ACT(junk2, xb, AF.Square, [("xn", tt % 2, 0), ("xn", tt % 2, 1)], ["junk2", ("ss2", tt)], accum_out=ss2[:, tt:tt + 1])
            ACT(sd2[:, tt:tt + 1], ss2[:, tt:tt + 1], AF.Sqrt, [("ss2", tt), "eps"], [("sd2", tt)], scale=1.0 / D, bias=eps_t[:, 0:1])
            RCP(r2[:, tt:tt + 1], sd2[:, tt:tt + 1], [("sd2", tt)], [("r2", tt)])
            ys = tt % NYT
            STT(yt[ys], xb, r2[:, tt:tt + 1], FG, ALU.mult, ALU.mult, [("xn", tt % 2, 0), ("xn", tt % 2, 1), ("r2", tt), "FG"], [("yt", ys)])
            DMA("sp", y[tt * 128:(tt + 1) * 128, :], yt[ys], f"yt{ys}", [("yt", ys)], [("yout", tt)])
        P.op("sp", None, [("yout", tt) for tt in range(NT)], ())
        finish()
    return nc


def _t5_bucket(n):
    max_exact = 16
    nf = np.maximum(n, 1).astype(np.float32)
    large = max_exact + (np.log(nf / np.float32(max_exact)) / np.float32(math.log(2048 / max_exact))
                         * np.float32(32 - max_exact)).astype(np.int32)
    large = np.minimum(large, 31)
    return np.where(n < max_exact, n, large)


def _const_tables(rel_bias):
    k = np.arange(128)[:, None]
    q = np.arange(128)[None, :]
    dist_prev = np.clip(128 + q - k, 0, 128)
    dist_cur = np.clip(q - k, 0, 128)
    ok_prev = np.broadcast_to(k >= q, (128, 128))
    ok_cur = np.broadcast_to(k <= q, (128, 128))
    bg = np.full((128, 12, 2, 2, 128), -1.0e4, np.float32)
    for gi, dil in enumerate(DILS):
        bp = _t5_bucket(dist_prev * dil)
        bc = _t5_bucket(dist_cur * dil)
        for hp in range(4):
            for hb in range(2):
                col = gi * 8 + hp * 2 + hb
                gc = rel_bias[bc, col]
                gp = rel_bias[bp, col]
                bg[:, hp * 3 + gi, hb, 0, :][ok_cur] = gc[ok_cur]
                bg[:, hp * 3 + gi, hb, 1, :][ok_prev] = gp[ok_prev]
    icnt = np.zeros((128, 4, 16), np.float32)
    for pg in range(4):
        win = 2 ** (pg + 1)
        icnt[:, pg, :] = 1.0 / np.minimum(np.arange(16) + 1, win).astype(np.float32)
    return bg.reshape(128, 12, 512), icnt.reshape(128, 64)


def make_in_maps(inputs):
    f = lambda a: np.ascontiguousarray(np.asarray(a, dtype=np.float32))
    x = f(inputs["x"])
    c = f(inputs["c"])
    bg, icnt = _const_tables(f(inputs["rel_bias"]))
    shared = {
        "ng": f(f(inputs["norm_g"])[0].reshape(8, 128).T),
        "w_ada": f(inputs["w_ada"])[0],
        "b_ada": f(inputs["b_ada"])[0].reshape(1, 3 * D),
        "w_in": f(inputs["w_in"])[0],
        "pool_w": f(inputs["pool_w"])[0],
        "pscale": f(f(inputs["pool_scale"])[0].reshape(4, 128).T),
        "w_ab": f(inputs["w_attn_br"])[0],
        "w_pb": f(inputs["w_pool_br"])[0],
        "w_out": f(inputs["w_out"])[0],
        "bg": bg, "icnt": icnt,
        "ident": np.eye(128, dtype=np.float32),
        "fg": f(np.broadcast_to(f(inputs["final_g"])[None, :], (128, D))),
    }
    maps = []
    for b in range(x.shape[0]):
        m = dict(shared)
        m["x"] = x[b]
        m["cT"] = f(c[b].reshape(8, 128).T)
        maps.append(m)
    return maps


_NC_CACHE = {}


def kernel(**inputs):
    maps = make_in_maps(inputs)
    if "nc" not in _NC_CACHE:
        _NC_CACHE["nc"] = build()
    res = run_bass_kernel_spmd(_NC_CACHE["nc"], maps, core_ids=list(range(8)))
    return np.stack([np.asarray(r["y"], dtype=np.float32) for r in res.results], axis=0)
```

```python
import math
from contextlib import ExitStack

import numpy as np
import concourse.bass as bass
import concourse.mybir as mybir
from concourse.bass_utils import run_bass_kernel_spmd

F32 = mybir.dt.float32
BF16 = mybir.dt.bfloat16
AF = mybir.ActivationFunctionType
ALU = mybir.AluOpType

S = 4096
D = 1024
NT = 32
NS = 8
EPS = 1e-6
DILS = (1, 4, 16)
EPOCH = 4000
SBUF_WORDS = 53200


class Op:
    __slots__ = ("eng", "fn", "slot", "slotn", "signal", "count", "deps", "waits")


class Prog:
    ENG = ("pe", "act", "dve", "pool", "sp")

    def __init__(self):
        self.ops = {e: [] for e in self.ENG}
        self.lastw = {}
        self.readers = {}
        self.slotcnt = {}
        self.lastdma = {}
        self.lastcomp = {}
        self.pending = {e: set() for e in self.ENG}

    def op(self, eng, fn, reads=(), writes=(), slot=None):
        o = Op()
        o.eng = eng
        o.fn = fn
        o.slot = slot
        o.signal = False
        o.count = 0
        o.slotn = 0
        deps = set(self.pending[eng])
        self.pending[eng] = set()
        for k in reads:
            w = self.lastw.get(k)
            if w is not None:
                deps.add(w)
        for k in writes:
            w = self.lastw.get(k)
            if w is not None:
                deps.add(w)
            rd = self.readers.get(k)
            if rd:
                for r in rd.values():
                    if isinstance(r, list):
                        deps.update(r)
                    else:
                        deps.add(r)
        for k in reads:
            rd = self.readers.setdefault(k, {})
            if slot is not None:
                rd.setdefault("_dma", []).append(o)
            else:
                rd[eng] = o
        for k in writes:
            self.lastw[k] = o
            self.readers[k] = {}
        deps.discard(o)
        o.deps = deps
        if slot is not None:
            n = self.slotcnt.get(slot, 0) + 1
            self.slotcnt[slot] = n
            o.slotn = n
            self.lastdma[slot] = o
        else:
            self.lastcomp[eng] = o
        self.ops[eng].append(o)
        return o

    def barrier(self):
        allops = set(self.lastcomp.values()) | set(self.lastdma.values())
        for e in self.ENG:
            self.pending[e] |= allops

    def finalize(self):
        for e in self.ENG:
            for o in self.ops[e]:
                for d in o.deps:
                    if d.slot is None and not (d.eng == "pe" and e == "pe"):
                        d.signal = True
        self.nsig = {}
        for e in self.ENG:
            c = 0
            for o in self.ops[e]:
                if o.slot is None and o.signal:
                    c += 1
                    o.count = c
            self.nsig[e] = c
        for e in self.ENG:
            seen = {}
            for o in self.ops[e]:
                w = {}
                for d in o.deps:
                    if d.slot is not None:
                        key = ("dma", d.slot)
                        val = d.slotn
                    else:
                        if d.eng == "pe" and e == "pe":
                            continue
                        key = ("eng", d.eng)
                        val = d.count
                    if seen.get(key, 0) >= val:
                        continue
                    if w.get(key, 0) < val:
                        w[key] = val
                for k, v in w.items():
                    seen[k] = v
                o.waits = w

    def emit(self, e, eng, sems):
        for o in self.ops[e]:
            for (kind, name), v in o.waits.items():
                if kind == "dma":
                    eng.wait_ge(sems["dma"][name], 16 * v)
                else:
                    ep = (v - 1) // EPOCH
                    eng.wait_ge(sems["eng"][name][ep], v - ep * EPOCH)
            if o.fn is None:
                continue
            inst = o.fn(eng)
            if o.slot is not None:
                inst.then_inc(sems["dma"][o.slot], 16)
            elif o.signal:
                ep = (o.count - 1) // EPOCH
                inst.then_inc(sems["eng"][e][ep], 1)


class Alloc:
    def __init__(self, big, nwords):
        self.big = big
        self.n = nwords
        self.top = 0
        self.hi = nwords

    def get_top(self, shape, dtype):
        save = self.top
        nel = 1
        for d in shape[1:]:
            nel *= d
        nw = ((nel * (4 if dtype == F32 else 2) + 3) // 4 + 7) // 8 * 8
        self.hi -= nw
        self.top = self.hi
        ap = self.get(shape, dtype, _chk=False)
        self.top = save
        assert self.top <= self.hi
        return ap

    def get(self, shape, dtype, _chk=True):
        nel = 1
        for d in shape[1:]:
            nel *= d
        nbytes = nel * (4 if dtype == F32 else 2)
        nw = (nbytes + 3) // 4
        off = self.top
        self.top += (nw + 7) // 8 * 8
        assert self.top <= (self.hi if _chk else self.n), f"SBUF overflow {self.top} > {self.hi}"
        ap = self.big[0:shape[0], off:off + nw]
        if dtype != F32:
            ap = ap.bitcast(dtype)
        if len(shape) > 2:
            names = "abcdef"[: len(shape) - 1]
            pat = "p (" + " ".join(names) + ") -> p " + " ".join(names)
            kw = {names[i]: shape[i + 1] for i in range(len(names))}
            ap = ap.rearrange(pat, **kw)
        return ap


def build(dbg=None):
    nc = bass.Bass("TRN2", target_bir_lowering=False)

    def din(name, shape):
        return nc.dram_tensor(name, shape, F32, kind="ExternalInput").ap()

    x = din("x", [S, D])
    cT_d = din("cT", [128, 8])
    ng_d = din("ng", [128, 8])
    wada_d = din("w_ada", [D, 3 * D])
    bada_d = din("b_ada", [1, 3 * D])
    win_d = din("w_in", [D, 8192])
    poolw_d = din("pool_w", [4, 128, 128])
    pscale_d = din("pscale", [128, 4])
    wab_d = din("w_ab", [512, D])
    wpb_d = din("w_pb", [512, D])
    wout_d = din("w_out", [D, D])
    bg_d = din("bg", [128, 12, 512])
    ident_d = din("ident", [128, 128])
    icnt_d = din("icnt", [128, 64])
    fg_d = din("fg", [128, D])
    y = nc.dram_tensor("y", [S, D], F32, kind="ExternalOutput").ap()
    dbg_d = None
    if dbg is not None:
        dshape = {"hT": ([128, 8 * S], BF16), "AT": ([128, 4 * S], BF16), "M2T": ([128, 4 * S], BF16),
                  "mg": ([128, 8 * S], BF16), "AT0": ([128, 4 * S], BF16), "acc": ([128, 2 * S], F32)}[dbg]
        dbg_d = nc.dram_tensor("dbg", dshape[0], dshape[1], kind="ExternalOutput").ap()

    win_v = win_d.rearrange("(kc p) f -> p kc f", p=128)

    P = Prog()
    es = ExitStack()
    with es:
        big = es.enter_context(nc.sbuf_tensor("big", [128, SBUF_WORDS], F32))
        psall = es.enter_context(nc.psum_tensor("psall", [128, 4096], F32))
        A = Alloc(big, SBUF_WORDS)

        def bank(b):
            return psall[:, b * 512:(b + 1) * 512]

        def bank_bf(b):
            return psall[:, b * 512:(b + 1) * 512].bitcast(BF16)

        def MM(out, lhsT, rhs, start, stop, reads, writes):
            P.op("pe", lambda e: e.matmul(out=out, lhsT=lhsT, rhs=rhs, start=start, stop=stop), reads, writes)

        def TR(out, in_, ident, reads, writes):
            P.op("pe", lambda e: e.transpose(out=out, in_=in_, identity=ident), reads, writes)

        def ACT(out, in_, func, reads, writes, scale=None, bias=None, accum_out=None):
            kw = {}
            if scale is not None:
                kw["scale"] = scale
            if bias is not None:
                kw["bias"] = bias
            if accum_out is not None:
                kw["accum_out"] = accum_out
            P.op("act", lambda e: e.activation(out=out, in_=in_, func=func, **kw), reads, writes)

        def TT(eng, out, in0, in1, op, reads, writes):
            P.op(eng, lambda e: e.tensor_tensor(out=out, in0=in0, in1=in1, op=op), reads, writes)

        def TS(eng, out, in0, s1, s2, op0, op1, reads, writes):
            if op1 is None:
                P.op(eng, lambda e: e.tensor_scalar(out=out, in0=in0, scalar1=s1, scalar2=None, op0=op0), reads, writes)
            else:
                P.op(eng, lambda e: e.tensor_scalar(out=out, in0=in0, scalar1=s1, scalar2=s2, op0=op0, op1=op1), reads, writes)

        def STT(out, in0, scalar, in1, op0, op1, reads, writes, accum_out=None):
            if accum_out is None:
                P.op("dve", lambda e: e.scalar_tensor_tensor(out=out, in0=in0, scalar=scalar, in1=in1, op0=op0, op1=op1), reads, writes)
            else:
                P.op("dve", lambda e: e.scalar_tensor_tensor(out=out, in0=in0, scalar=scalar, in1=in1, op0=op0, op1=op1,
                                                             accum_out=accum_out), reads, writes)

        def CP(eng, out, in_, reads, writes):
            if eng == "act":
                ACT(out, in_, AF.Copy, reads, writes)
            else:
                P.op(eng, lambda e: e.tensor_copy(out=out, in_=in_), reads, writes)

        def RCP(out, in_, reads, writes):
            P.op("dve", lambda e: e.reciprocal(out=out, in_=in_), reads, writes)

        def MSET(eng, ap, val, writes):
            P.op(eng, lambda e: e.memset(ap, val), (), writes)

        def DMA(q, out, in_, slot, reads, writes):
            P.op(q, lambda e: e.dma_start(out=out, in_=in_), reads, writes, slot=slot)

        hT = A.get([128, 8, S], BF16)
        GB = A.get([128, D], F32)
        ident = A.get([128, 128], BF16)
        ones_bf = A.get([128, 64], BF16)
        ones_f = A.get([1, 128], F32)
        eps_t = A.get([128, 1], F32)
        cT = A.get([128, 8], F32)
        ng = A.get([128, 8], F32)
        gs = A.get([128, 8], F32)
        shc = A.get([128, 8], F32)
        ss = A.get([128, NT], F32)
        sd = A.get([128, NT], F32)
        rstd = A.get([128, NT], F32)
        persist_top = A.top

        DMA("pool", ident, ident_d, "c_ident", (), ["ident"])
        MSET("pool", ones_bf, 1.0, ["ones_bf"])
        MSET("pool", ones_f, 1.0, ["ones_f"])
        MSET("pool", eps_t, EPS, ["eps"])
        DMA("sp", cT, cT_d, "c_cT", (), ["cT"])
        DMA("sp", ng, ng_d, "c_ng", (), ["ng"])

        wa = [A.get([128, 2 * D], F32) for _ in range(3)]
        wag = [A.get([128, D], F32) for _ in range(4)]
        bada = A.get([1, 3 * D], F32)
        mod = A.get([1, 3 * D], F32)
        wu = [A.get_top([128, 3, 8, 128], BF16) for _ in range(2)]
        wzb = [A.get_top([128, 8, 128], BF16) for _ in range(2)]
        Etab = [A.get_top([128, 3, 512], BF16) for _ in range(2)]
        bst = A.get_top([128, 512], F32)
        xt = [A.get([128, D], F32) for _ in range(8)]
        xh = [A.get([128, D], BF16) for _ in range(8)]
        junk = A.get([128, D], BF16)

        DMA("sp", bada, bada_d, "c_bada", (), ["bada"])
        def load_wa(kc):
            DMA("sp", wa[kc % 3], wada_d[kc * 128:(kc + 1) * 128, 0:2 * D], f"wa{kc % 3}", (), [("wa", kc % 3)])

        def load_xt(t):
            DMA("sp", xt[t % 8], x[t * 128:(t + 1) * 128, :], f"xt{t % 8}", (), [("xt", t % 8)])

        for kc in range(3):
            load_wa(kc)
        for t in range(8):
            load_xt(t)
        for kc in range(8):
            if kc >= 3:
                load_wa(kc)
            for n in range(4):
                MM(bank(n)[0:1, :], cT[:, kc:kc + 1], wa[kc % 3][:, n * 512:(n + 1) * 512], kc == 0, kc == 7,
                   [("wa", kc % 3), "cT"], [("ps", n)])
        def load_wag(kc):
            DMA("pool", wag[kc % 4], wada_d[kc * 128:(kc + 1) * 128, 2 * D:3 * D], f"wag{kc % 4}", (), [("wag", kc % 4)])

        for kc in range(4):
            load_wag(kc)
        for n in range(4):
            TT("dve", mod[0:1, n * 512:(n + 1) * 512], bank(n)[0:1, :], bada[0:1, n * 512:(n + 1) * 512], ALU.add,
               [("ps", n), "bada"], [("mod", n)])
        for j in range(16):
            MM(bank(6)[:, j:j + 1], mod[0:1, j * 128:(j + 1) * 128], ones_f[0:1, 0:1], True, True,
               [("mod", j // 4), "ones_f"], [("ps", 6)])
        CP("dve", shc, bank(6)[:, 0:8], [("ps", 6)], ["shc"])
        STT(gs, bank(6)[:, 8:16], 1.0, ng, ALU.add, ALU.mult, [("ps", 6), "ng"], ["gs"])

        def gate_mm(kc):
            for n in range(2):
                MM(bank(n)[0:1, :], cT[:, kc:kc + 1], wag[kc % 4][:, n * 512:(n + 1) * 512], kc == 0, kc == 7,
                   [("wag", kc % 4), "cT"], [("ps", n)])

        def gate_part():
            for n in range(2):
                TT("dve", mod[0:1, 2048 + n * 512:2048 + (n + 1) * 512], bank(n)[0:1, :],
                   bada[0:1, 2048 + n * 512:2048 + (n + 1) * 512], ALU.add, [("ps", n), "bada"], [("mod", 4 + n)])
            for n in range(2):
                MM(bank(2 + n), ones_f[0:1, 0:128], mod[0:1, 2048 + n * 512:2048 + (n + 1) * 512], True, True,
                   [("mod", 4 + n), "ones_f"], [("ps", 2 + n)])
                CP("dve", GB[:, n * 512:(n + 1) * 512], bank(2 + n), [("ps", 2 + n)], [("GB", n)])

        def load_unit_weights(u):
            hp_, gi_ = divmod(u, 3)
            par_ = u % 2
            for j in range(3):
                c0 = gi_ * 1536 + j * 512 + hp_ * 128
                DMA("pool", wu[par_][:, j, :, :], win_v[:, :, c0:c0 + 128], f"wu{par_}{j}", (), [("wu", par_, j)])

        def load_hp_tables(hp_):
            for gi_ in range(3):
                DMA("pool", bst, bg_d[:, hp_ * 3 + gi_, :], "bst", (), ["bst"])
                ACT(Etab[hp_ % 2][:, gi_, :], bst, AF.Exp, ["bst"], [("E", hp_ % 2, gi_)])
            c0z = 4608 + hp_ * 128
            DMA("pool", wzb[hp_ % 2], win_v[:, :, c0z:c0z + 128], f"wz{hp_ % 2}", (), [("wz", hp_ % 2)])

        load_unit_weights(0)
        load_hp_tables(0)

        trbanks = (7, 5, 4, 6)
        tri = [0]

        def stageA(grp):
            for i in range(4):
                t = grp * 4 + i
                sl = t % 8
                if t >= 8:
                    load_xt(t)
                STT(junk, xt[sl], 1.0, xt[sl], ALU.mult, ALU.mult, [("xt", sl)], ["junk", ("ss", t)],
                    accum_out=ss[:, t:t + 1])
            g4 = slice(grp * 4, grp * 4 + 4)
            ACT(sd[:, g4], ss[:, g4], AF.Sqrt, [("ss", grp * 4 + i) for i in range(4)] + ["eps"], [("sd", grp)],
                scale=1.0 / D, bias=eps_t[:, 0:1])
            RCP(rstd[:, g4], sd[:, g4], [("sd", grp)], [("rstd", grp)])
            for i in range(4):
                t = grp * 4 + i
                sl = t % 8
                xi = (grp % 2) * 4 + i
                if i < 2:
                    ACT(xh[xi], xt[sl], AF.Identity, [("xt", sl), ("rstd", grp)], [("xh", xi)], scale=rstd[:, t:t + 1])
                else:
                    TS("dve", xh[xi], xt[sl], rstd[:, t:t + 1], None, ALU.mult, None,
                       [("xt", sl), ("rstd", grp)], [("xh", xi)])

        def stageB(grp):
            for k2 in range(4):
                b = trbanks[tri[0] % 4]
                tri[0] += 1
                trb = bank_bf(b)
                for kcl in range(2):
                    kc = k2 * 2 + kcl
                    for i in range(4):
                        xi = (grp % 2) * 4 + i
                        TR(trb[:, kcl * 512 + i * 128:kcl * 512 + (i + 1) * 128], xh[xi][:, kc * 128:(kc + 1) * 128], ident,
                           [("xh", xi), "ident"], [("ps", b)])
                for kcl in range(2):
                    kc = k2 * 2 + kcl
                    if k2 < 3:
                        ACT(hT[:, kc, grp * 512:(grp + 1) * 512], trb[:, kcl * 512:(kcl + 1) * 512], AF.Identity,
                            [("ps", b), "gs", "shc"], [("hT", kc, grp)], scale=gs[:, kc:kc + 1], bias=shc[:, kc:kc + 1])
                    else:
                        TS("dve", hT[:, kc, grp * 512:(grp + 1) * 512], trb[:, kcl * 512:(kcl + 1) * 512],
                           gs[:, kc:kc + 1], shc[:, kc:kc + 1], ALU.mult, ALU.add,
                           [("ps", b), "gs", "shc"], [("hT", kc, grp)])
            gate_mm(grp)
            if grp + 4 < 8:
                load_wag(grp + 4)
            if grp == 7:
                gate_part()

        stageA(0)
        for grp in range(8):
            if grp + 1 < 8:
                stageA(grp + 1)
            stageB(grp)

        def dump(src2d, name):
            P.barrier()
            DMA("sp", dbg_d, src2d, "dbg", [name], ["dbgout"])
            P.op("sp", None, ["dbgout"], ())

        def finish():
            P.finalize()
            sems = {"dma": {}, "eng": {}}
            for slot in P.slotcnt:
                sems["dma"][slot] = es.enter_context(nc.semaphore("d_" + str(slot)))
            for e in P.ENG:
                nep = max(1, (P.nsig[e] + EPOCH - 1) // EPOCH)
                sems["eng"][e] = [es.enter_context(nc.semaphore(f"e_{e}{i}")) for i in range(nep)]
            block = es.enter_context(nc.Block())

            @block.tensor
            def _(eng):
                P.emit("pe", eng, sems)

            @block.scalar
            def _(eng):
                P.emit("act", eng, sems)

            @block.vector
            def _(eng):
                P.emit("dve", eng, sems)

            @block.gpsimd
            def _(eng):
                P.emit("pool", eng, sems)

            @block.sync
            def _(eng):
                P.emit("sp", eng, sems)

        if dbg == "hT":
            P.lastw["hTall"] = None
            dump(hT.rearrange("p a b -> p (a b)"), "hTall")
            finish()
            return nc

        P.barrier()
        A.top = persist_top
        AT = A.get([128, 4, S], BF16)
        p2_top = A.top
        QT = A.get([128, S], BF16)
        KT = A.get([128, S], BF16)
        Vaug = A.get([128, 32, 2, 128], BF16)
        VT = A.get([128, 2048], BF16)
        acc = A.get([128, 2, S], F32)
        Xe = [A.get([128, 2, 256], BF16) for _ in range(2)]
        PT = [A.get([128, 2, 256], BF16) for _ in range(4)]
        zs = A.get([128, 512], F32)
        rec = [A.get([128, 512], F32) for _ in range(2)]

        MSET("pool", Vaug.rearrange("p a b c -> p (a b c)"), 1.0, ["Vones"] + [(("V", k_), h_) for k_ in range(32) for h_ in range(2)])

        PROJ_BANKS = (0, 1, 6, 7)
        STA = (0, 1)
        STB = (2, 3)
        NDB = (4, 5, 6, 7)
        TRB = (2, 3, 4, 5)
        pj = [0]

        def next_proj_bank():
            b = PROJ_BANKS[pj[0] % len(PROJ_BANKS)]
            pj[0] += 1
            return b

        evi = [0]

        def evac_engine():
            evi[0] += 1
            return "act" if evi[0] % 2 == 0 else "dve"

        def finish_slice(pf, s):
            fhp, akeys, fkeys = pf
            wz = wzb[fhp % 2]
            b = next_proj_bank()
            for kc in range(8):
                MM(bank(b), wz[:, kc, :], hT[:, kc, s * 512:(s + 1) * 512], kc == 0, kc == 7,
                   [("wz", fhp % 2), ("hT", kc, s)], [("ps", b)])
            ACT(zs, bank(b), AF.Silu, [("ps", b)], ["zs"])
            sl = slice(s * 512, (s + 1) * 512)
            rc = rec[s % 2]
            ra = ("rec", s % 2, 0)
            rb = ("rec", s % 2, 1)
            DMA("sp", rc[0:64, :], acc[64:128, 0, sl], f"rcA{s % 2}", akeys, [ra, ("accF", fhp, s, 0)])
            DMA("sp", rc[64:128, :], acc[0:64, 1, sl], f"rcB{s % 2}", akeys, [rb, ("accF", fhp, s, 1)])
            RCP(rc, rc, [ra, rb], [ra, rb])
            TT("dve", rc[0:64, :], rc[0:64, :], acc[0:64, 0, sl], ALU.mult, [ra] + akeys, [ra, ("accF", fhp, s, 2)])
            TT("dve", rc[64:128, :], rc[64:128, :], acc[64:128, 1, sl], ALU.mult, [rb] + akeys, [rb, ("accF", fhp, s, 3)])
            TT("pool", AT[:, fhp, sl], rc, zs, ALU.mult, [ra, rb, "zs"], [("AT", fhp)])
            fkeys += [("accF", fhp, s, k) for k in range(4)]

        hps = list(range(4)) if dbg != "AT0" else [0]
        acc_prev = []
        pend_fin = None
        for hp in hps:
            epar = hp % 2
            if hp > 0:
                load_hp_tables(hp)
            for gi in range(3):
                u = hp * 3 + gi
                par = u % 2
                dil = DILS[gi]
                nb = 32 // dil
                Ls = 512 // dil
                if u + 1 < 12 and not (dbg == "AT0" and u + 1 >= 3):
                    load_unit_weights(u + 1)
                for s in range(NS):
                    if pend_fin is not None and gi == 0:
                        finish_slice(pend_fin, s)
                    for j in range(3):
                        b = next_proj_bank()
                        for kc in range(8):
                            MM(bank(b), wu[par][:, j, kc, :], hT[:, kc, s * 512:(s + 1) * 512], kc == 0, kc == 7,
                               [("wu", par, j), ("hT", kc, s)], [("ps", b)])
                        if dil == 1:
                            src = bank(b)
                        else:
                            src = bank(b).rearrange("p (l r) -> p r l", r=dil)
                        if j < 2:
                            T = QT if j == 0 else KT
                            if dil == 1:
                                dst = T[:, s * 512:(s + 1) * 512]
                            else:
                                dst = T.rearrange("p (r l) -> p r l", r=dil)[:, :, s * Ls:(s + 1) * Ls]
                            CP(evac_engine(), dst, src, [("ps", b)], [("QT" if j == 0 else "KT", s)])
                        else:
                            if dil == 1:
                                dst = VT[:, (s % 4) * 512:(s % 4 + 1) * 512]
                            else:
                                dst = VT.rearrange("p (r l) -> p r l", r=dil)[:, :, (s % 4) * Ls:(s % 4 + 1) * Ls]
                            CP(evac_engine(), dst, src, [("ps", b)], [("VT", s % 4)])
                            Lv = 2048 // dil
                            if dil == 1:
                                blks = [[(0, 4 * s + i) for i in range(4)]]
                            elif dil == 4:
                                blks = [[(r, s) for r in range(4)]]
                            else:
                                blks = [[(r0 + i, s // 4) for i in range(4)] for r0 in range(0, 16, 4)] if s % 4 == 3 else []
                            for bl in blks:
                                tb = TRB[(s + bl[0][0] // 4) % 4]
                                trb = bank_bf(tb)
                                for i, (r, n) in enumerate(bl):
                                    off = r * Lv + (n * 128) % Lv
                                    TR(trb[:, i * 128:(i + 1) * 128], VT[:, off:off + 128], ident,
                                       [("VT", q) for q in range(4)] + ["ident"], [("ps", tb)])
                                kb0 = bl[0][0] * nb + bl[0][1]
                                kstep = (bl[1][0] * nb + bl[1][1]) - kb0
                                kbs = slice(kb0, kb0 + 3 * kstep + 1, kstep)
                                srcv = trb[:, 0:512].rearrange("p (a b) -> p a b", a=4)
                                vkeys = [("V", bl[i][0] * nb + bl[i][1]) for i in range(4)]
                                ve = evac_engine()
                                CP(ve, Vaug[:, kbs, 0, 0:64], srcv[:, :, 0:64], [("ps", tb)], [(k, 0) for k in vkeys])
                                CP(ve, Vaug[:, kbs, 1, 64:128], srcv[:, :, 64:128], [("ps", tb)], [(k, 1) for k in vkeys])
                if pend_fin is not None and gi == 0:
                    acc_prev = pend_fin[2]
                    pend_fin = None
                blocks = [(r, n) for r in range(dil) for n in range(nb)]
                NBk = len(blocks)
                qk_reads = [("QT", s_) for s_ in range(NS)] + [("KT", s_) for s_ in range(NS)]
                ev = Etab[epar][:, gi, :].rearrange("p (h c) -> p h c", h=2)
                accv = acc.rearrange("p c (l r) -> p c r l", r=dil)
                cur_keys = []

                def Nof(i):
                    return 256 if blocks[i][1] < nb - 1 else 128

                def S_(i):
                    N = Nof(i)
                    for hb, banks in ((0, STA), (1, STB)):
                        b = banks[i % 2]
                        rows = slice(hb * 64, (hb + 1) * 64)
                        MM(bank(b)[:, 0:N], KT[rows, i * 128:(i + 1) * 128], QT[rows, i * 128:i * 128 + N],
                           True, True, qk_reads, [("ps", b)])

                def X_(i):
                    N = Nof(i)
                    for hb, banks in ((0, STA), (1, STB)):
                        b = banks[i % 2]
                        ACT(Xe[i % 2][:, hb, 0:N], bank(b)[:, 0:N], AF.Exp, [("ps", b)], [("Xe", i % 2, hb)], scale=0.125)

                def M_(i):
                    N = Nof(i)
                    TT("dve", PT[i % 4][:, :, 0:N], Xe[i % 2][:, :, 0:N], ev[:, :, 0:N], ALU.mult,
                       [("Xe", i % 2, 0), ("Xe", i % 2, 1), ("E", epar, gi)], [("PT", i % 4)])

                def V_(i):
                    r, n = blocks[i]
                    nd = NDB[i % 4]
                    for hb in range(2):
                        cols = slice(hb * 128, (hb + 1) * 128)
                        if n > 0:
                            MM(bank(nd)[:, cols], Vaug[:, i - 1, hb, :], PT[(i - 1) % 4][:, hb, 128:256], True, False,
                               [(("V", i - 1), hb), "Vones", ("PT", (i - 1) % 4)], [("ps", nd)])
                        MM(bank(nd)[:, cols], Vaug[:, i, hb, :], PT[i % 4][:, hb, 0:128], n == 0, True,
                           [(("V", i), hb), "Vones", ("PT", i % 4)], [("ps", nd)])

                def A_(i):
                    r, n = blocks[i]
                    nd = NDB[i % 4]
                    dstv = accv[:, :, r, n * 128:(n + 1) * 128]
                    srcv = bank(nd)[:, 0:256].rearrange("p (c q) -> p c q", c=2)
                    key = ("accA", u, i)
                    cur_keys.append(key)
                    if gi == 0:
                        CP("act", dstv, srcv, [("ps", nd)] + acc_prev, [key])
                    else:
                        TT("dve", dstv, srcv, dstv, ALU.add, [("ps", nd)] + acc_prev, [key])

                NBe = NBk
                for step in range(NBe + 4):
                    if step < NBe:
                        S_(step)
                        X_(step)
                        M_(step)
                    if 0 <= step - 2 < NBe:
                        V_(step - 2)
                    if 0 <= step - 4 < NBe:
                        A_(step - 4)
                acc_prev = cur_keys

            if dbg == "acc":
                dump(acc.rearrange("p a b -> p (a b)"), "acc")
                finish()
                return nc
            pend_fin = (hp, list(acc_prev), [])
            if hp == hps[-1]:
                for s_ in range(NS):
                    finish_slice(pend_fin, s_)
                acc_prev = pend_fin[2]
                pend_fin = None

        if dbg in ("AT", "AT0"):
            P.lastw["ATall"] = None
            dump(AT.rearrange("p a b -> p (a b)"), "ATall")
            finish()
            return nc

        P.barrier()
        A.top = p2_top
        A.hi = A.n
        M2T = A.get([128, 4, S], BF16)
        p3_top = A.top
        Up = A.get([128, 16 + S], F32)
        Ta = A.get([128, 16 + S], F32)
        Tb = A.get([128, 16 + S], F32)
        pooled = A.get([128, S], BF16)
        wu2 = [A.get([128, 2, 8, 128], BF16) for _ in range(2)]
        pw = A.get([128, 4, 128], BF16)
        pscale = A.get([128, 4], F32)
        icnt = A.get([128, 64], F32)
        zs3 = [A.get([128, 512], BF16) for _ in range(4)]
        t16 = A.get([128, 16], F32)

        DMA("sp", pscale, pscale_d, "c_pscale", (), ["pscale"])
        DMA("sp", icnt, icnt_d, "c_icnt", (), ["icnt"])
        DMA("pool", pw, poolw_d.rearrange("g c e -> c g e"), "c_pw", (), ["pw"])
        MSET("pool", Up[:, 0:16], 0.0, ["pad"])
        MSET("pool", Ta[:, 0:16], 0.0, ["pad"])
        MSET("pool", Tb[:, 0:16], 0.0, ["pad"])

        def load_pool_weights(pg):
            for j, base in enumerate((5120, 5632)):
                c0 = base + pg * 128
                DMA("pool", wu2[pg % 2][:, j, :, :], win_v[:, :, c0:c0 + 128], f"wu2{pg % 2}{j}", (), [("wu2", pg % 2, j)])

        load_pool_weights(0)
        LA = 3

        def uproj(pg, s):
            w2 = wu2[pg % 2]
            b = next_proj_bank()
            for kc in range(8):
                MM(bank(b), w2[:, 0, kc, :], hT[:, kc, s * 512:(s + 1) * 512], kc == 0, kc == 7,
                   [("wu2", pg % 2, 0), ("hT", kc, s)], [("ps", b)])
            CP(evac_engine(), Up[:, 16 + s * 512:16 + (s + 1) * 512], bank(b), [("ps", b)], [("Up", s)])

        for s in range(NS):
            uproj(0, s)
        upk = [("Up", s_) for s_ in range(NS)]
        for pg in range(4):
            if pg + 1 < 4:
                load_pool_weights(pg + 1)
            w2 = wu2[pg % 2]
            win = 2 ** (pg + 1)
            a, akeys = Up, upk
            for k in range(pg + 1):
                bbuf, bname = (Ta, "Ta") if k % 2 == 0 else (Tb, "Tb")
                sh = 2 ** k
                TT("dve", bbuf[:, 16:16 + S], a[:, 16:16 + S], a[:, 16 - sh:16 - sh + S], ALU.add, akeys + ["pad"], [bname])
                a, akeys = bbuf, [bname]
            STT(pooled, a[:, 16:16 + S], 1.0 / win, Up[:, 16:16 + S], ALU.mult, ALU.subtract, akeys + upk, ["pooled"])
            TT("dve", t16, a[:, 16:32], icnt[:, pg * 16:(pg + 1) * 16], ALU.mult, akeys + ["icnt"], ["t16"])
            TT("dve", pooled[:, 0:16], t16, Up[:, 16:32], ALU.subtract, ["t16", "pooled"] + upk, ["pooled"])
            for step in range(NS + LA):
                if step < NS:
                    s = step
                    sl = slice(s * 512, (s + 1) * 512)
                    b2 = 2 + s % 4
                    for kc in range(8):
                        MM(bank(b2), w2[:, 1, kc, :], hT[:, kc, sl], kc == 0, kc == 7,
                           [("wu2", pg % 2, 1), ("hT", kc, s)], [("ps", b2)])
                    ACT(zs3[s % 4], bank(b2), AF.Silu, [("ps", b2)], [("zs3", s % 4)])
                if step >= LA:
                    s = step - LA
                    sl = slice(s * 512, (s + 1) * 512)
                    b1 = next_proj_bank()
                    MM(bank(b1), pw[:, pg, :], pooled[:, sl], True, True, ["pw", "pooled"], [("ps", b1)])
                    STT(M2T[:, pg, sl], bank(b1), pscale[:, pg:pg + 1], zs3[s % 4], ALU.mult, ALU.mult,
                        [("ps", b1), ("zs3", s % 4), "pscale"], [("M2T", pg)])
                    if pg + 1 < 4:
                        uproj(pg + 1, s)

        if dbg == "M2T":
            P.lastw["M2Tall"] = None
            dump(M2T.rearrange("p a b -> p (a b)"), "M2Tall")
            finish()
            return nc

        P.barrier()
        A.top = p3_top
        wg = A.get([128, 2, 8, D], BF16)
        wab = A.get([128, 4, D], BF16)
        wpb = A.get([128, 4, D], BF16)
        p4_top = A.top
        sg = A.get([128, 2, 8, 512], BF16)
        t1 = [A.get([128, 512], F32) for _ in range(2)]
        t2 = [A.get([128, 512], F32) for _ in range(2)]

        for fb in range(4):
            for wsel in range(2):
                c0g = 6144 + wsel * 1024 + fb * 256
                DMA("pool", wg[:, wsel, :, fb * 256:(fb + 1) * 256], win_v[:, :, c0g:c0g + 256],
                    f"wg{wsel}{fb}", (), [("wg", wsel, fb)])
        DMA("pool", wab, wab_d.rearrange("(c p) f -> p c f", p=128), "c_wab", (), ["wab"])
        DMA("pool", wpb, wpb_d.rearrange("(c p) f -> p c f", p=128), "c_wpb", (), ["wpb"])

        G_BANKS = (0, 1, 2, 3)
        Y_BANKS = (4, 5, 6, 7)
        gi_ = 0
        yi_ = 0
        for s in range(NS):
            sl = slice(s * 512, (s + 1) * 512)
            for f in range(8):
                for wsel in range(2):
                    b = G_BANKS[gi_ % 4]
                    gi_ += 1
                    for kc in range(8):
                        MM(bank(b), wg[:, wsel, kc, f * 128:(f + 1) * 128], hT[:, kc, sl], kc == 0, kc == 7,
                           [("wg", wsel, f // 2), ("hT", kc, s)], [("ps", b)])
                    ACT(sg[:, wsel, f, :], bank(b), AF.Sigmoid, [("ps", b)], [("sg", wsel, f)])
            for f in range(8):
                ba = Y_BANKS[yi_ % 4]
                bp = Y_BANKS[(yi_ + 1) % 4]
                yi_ += 2
                for c in range(4):
                    MM(bank(ba), wab[:, c, f * 128:(f + 1) * 128], AT[:, c, sl], c == 0, c == 3,
                       ["wab", ("AT", c)], [("ps", ba)])
                for c in range(4):
                    MM(bank(bp), wpb[:, c, f * 128:(f + 1) * 128], M2T[:, c, sl], c == 0, c == 3,
                       ["wpb", ("M2T", c)], [("ps", bp)])
                TT("dve", t1[f % 2], bank(ba), sg[:, 0, f, :], ALU.mult, [("ps", ba), ("sg", 0, f)], [("t1", f % 2)])
                TT("dve", t2[f % 2], bank(bp), sg[:, 1, f, :], ALU.mult, [("ps", bp), ("sg", 1, f)], [("t2", f % 2)])
                TT("pool", hT[:, f, sl], t1[f % 2], t2[f % 2], ALU.add, [("t1", f % 2), ("t2", f % 2)], [("hT", f, s)])

        if dbg == "mg":
            P.lastw["hTall"] = None
            dump(hT.rearrange("p a b -> p (a b)"), "hTall")
            finish()
            return nc

        P.barrier()
        A.top = p2_top
        wout = A.get([128, 8, D], BF16)
        FG = A.get([128, D], F32)
        NX2 = 8
        NYT = 6
        x2 = [A.get([128, D], F32) for _ in range(NX2)]
        xn = [A.get([128, D], F32) for _ in range(4)]
        yt = [A.get([128, D], F32) for _ in range(NYT)]
        junk2 = A.get([128, D], BF16)
        ss2 = A.get([128, NT], F32)
        sd2 = A.get([128, NT], F32)
        r2 = A.get([128, NT], F32)

        for m in range(8):
            DMA("pool", wout[:, m, :], wout_d[m * 128:(m + 1) * 128, :], f"c_wout{m}", (), [("wout", m)])
        for m in range(8):
            TT("dve", wout[:, m, :], wout[:, m, :], GB, ALU.mult, [("wout", m), ("GB", 0), ("GB", 1)], [("wout", m)])
        DMA("sp", FG, fg_d, "c_fg", (), ["FG"])
        outs = []
        OB = ((0, 1), (2, 3), (4, 5), (6, 7))
        def load_x2(t_):
            DMA("sp", x2[t_ % NX2], x[t_ * 128:(t_ + 1) * 128, :], f"x2{t_ % NX2}", (), [("x2", t_ % NX2)])

        for t_ in range(NX2 - 1):
            load_x2(t_)
        for tt in range(NT):
            s = tt // 4
            xs = tt % NX2
            if tt + NX2 - 1 < NT:
                load_x2(tt + NX2 - 1)
            xb = xn[tt % 4]
            for half in range(2):
                b = OB[tt % 4][half]
                for m in range(8):
                    MM(bank(b), hT[:, m, tt * 128:(tt + 1) * 128], wout[:, m, half * 512:(half + 1) * 512], m == 0, m == 7,
                       [("hT", m, s), ("wout", m)], [("ps", b)])
                TT("dve", xb[:, half * 512:(half + 1) * 512], bank(b), x2[xs][:, half * 512:(half + 1) * 512], ALU.add,
                   [("ps", b), ("x2", xs)], [("xn", tt % 4, half)])
            ACT(junk2, xb, AF.Square, [("xn", tt % 4, 0), ("xn", tt % 4, 1)], ["junk2", ("ss2", tt)], accum_out=ss2[:, tt:tt + 1])
            ACT(sd2[:, tt:tt + 1], ss2[:, tt:tt + 1], AF.Sqrt, [("ss2", tt), "eps"], [("sd2", tt)], scale=1.0 / D, bias=eps_t[:, 0:1])
            RCP(r2[:, tt:tt + 1], sd2[:, tt:tt + 1], [("sd2", tt)], [("r2", tt)])
            ys = tt % NYT
            STT(yt[ys], xb, r2[:, tt:tt + 1], FG, ALU.mult, ALU.mult, [("xn", tt % 4, 0), ("xn", tt % 4, 1), ("r2", tt), "FG"], [("yt", ys)])
            DMA("sp", y[tt * 128:(tt + 1) * 128, :], yt[ys], f"yt{ys}", [("yt", ys)], [("yout", tt)])
        P.op("sp", None, [("yout", tt) for tt in range(NT)], ())
        finish()
    return nc


def _t5_bucket(n):
    max_exact = 16
    nf = np.maximum(n, 1).astype(np.float32)
    large = max_exact + (np.log(nf / np.float32(max_exact)) / np.float32(math.log(2048 / max_exact))
                         * np.float32(32 - max_exact)).astype(np.int32)
    large = np.minimum(large, 31)
    return np.where(n < max_exact, n, large)


def _const_tables(rel_bias):
    k = np.arange(128)[:, None]
    q = np.arange(128)[None, :]
    dist_prev = np.clip(128 + q - k, 0, 128)
    dist_cur = np.clip(q - k, 0, 128)
    ok_prev = np.broadcast_to(k >= q, (128, 128))
    ok_cur = np.broadcast_to(k <= q, (128, 128))
    bg = np.full((128, 12, 2, 2, 128), -1.0e4, np.float32)
    for gi, dil in enumerate(DILS):
        bp = _t5_bucket(dist_prev * dil)
        bc = _t5_bucket(dist_cur * dil)
        for hp in range(4):
            for hb in range(2):
                col = gi * 8 + hp * 2 + hb
                gc = rel_bias[bc, col]
                gp = rel_bias[bp, col]
                bg[:, hp * 3 + gi, hb, 0, :][ok_cur] = gc[ok_cur]
                bg[:, hp * 3 + gi, hb, 1, :][ok_prev] = gp[ok_prev]
    icnt = np.zeros((128, 4, 16), np.float32)
    for pg in range(4):
        win = 2 ** (pg + 1)
        icnt[:, pg, :] = 1.0 / np.minimum(np.arange(16) + 1, win).astype(np.float32)
    return bg.reshape(128, 12, 512), icnt.reshape(128, 64)


def make_in_maps(inputs):
    f = lambda a: np.ascontiguousarray(np.asarray(a, dtype=np.float32))
    x = f(inputs["x"])
    c = f(inputs["c"])
    bg, icnt = _const_tables(f(inputs["rel_bias"]))
    shared = {
        "ng": f(f(inputs["norm_g"])[0].reshape(8, 128).T),
        "w_ada": f(inputs["w_ada"])[0],
        "b_ada": f(inputs["b_ada"])[0].reshape(1, 3 * D),
        "w_in": f(inputs["w_in"])[0],
        "pool_w": f(inputs["pool_w"])[0],
        "pscale": f(f(inputs["pool_scale"])[0].reshape(4, 128).T),
        "w_ab": f(inputs["w_attn_br"])[0],
        "w_pb": f(inputs["w_pool_br"])[0],
        "w_out": f(inputs["w_out"])[0],
        "bg": bg, "icnt": icnt,
        "ident": np.eye(128, dtype=np.float32),
        "fg": f(np.broadcast_to(f(inputs["final_g"])[None, :], (128, D))),
    }
    maps = []
    for b in range(x.shape[0]):
        m = dict(shared)
        m["x"] = x[b]
        m["cT"] = f(c[b].reshape(8, 128).T)
        maps.append(m)
    return maps


_NC_CACHE = {}


def kernel(**inputs):
    maps = make_in_maps(inputs)
    if "nc" not in _NC_CACHE:
        _NC_CACHE["nc"] = build()
    res = run_bass_kernel_spmd(_NC_CACHE["nc"], maps, core_ids=list(range(8)))
    return np.stack([np.asarray(r["y"], dtype=np.float32) for r in res.results], axis=0)
```
